# Optimizing a Trainium2 kernel written in Bass

```python
import math
import jax, jax.numpy as jnp
from jax import lax
import numpy as np

D_MODEL = 2048
BATCH = 4
SEQ = 2048
DEPTH = 1
DEC_BATCH = 4
DEC_SEQ = 8192
PAST_LEN = 128

N_MEM = 256
D_MIX = D_MODEL
EPS = 1e-6
SSD_WIDTH = D_MIX // 2
SSD_HEAD_DIM = 64
SSD_HEADS = SSD_WIDTH // SSD_HEAD_DIM
SSD_GROUPS = 2
SSD_STATE = 128
SSD_CHUNK = 128
D_CONV = 5
CONV_CH = SSD_WIDTH + 2 * SSD_GROUPS * SSD_STATE
DT_MIN = 1e-3
DT_MAX = 1e-1
ATT_WIDTH = D_MIX // 4
ATT_HEAD_DIM = 64
ATT_HEADS = ATT_WIDTH // ATT_HEAD_DIM
ATT_KV_HEADS = 2
WINDOW = 128
BLOCK = 128
MEM_WIDTH = D_MIX - SSD_WIDTH - ATT_WIDTH
MEM_HEADS = 4
MEM_HEAD_DIM = MEM_WIDTH // MEM_HEADS
IN_SIZES = (CONV_CH, SSD_WIDTH, 2 * SSD_HEADS,
            ATT_HEADS * ATT_HEAD_DIM, ATT_KV_HEADS * ATT_HEAD_DIM, ATT_KV_HEADS * ATT_HEAD_DIM, ATT_WIDTH,
            MEM_WIDTH, MEM_WIDTH)
D_IN = sum(IN_SIZES)

kernel_name = 'hymba_bidir_ssd_swa_mem_encoder'


def rms_norm(x, g):
    xf = x.astype(jnp.float32)
    y = xf * lax.rsqrt(jnp.mean(xf * xf, axis=-1, keepdims=True) + EPS)
    return (y * g.astype(jnp.float32)).astype(x.dtype)


def split_cols(x, sizes):
    idx = np.cumsum(sizes)[:-1].tolist()
    return jnp.split(x, idx, axis=-1)


def centred_dwconv(u, w, b):
    K = w.shape[0]
    out = lax.conv_general_dilated(u, w[:, None, :].astype(u.dtype), window_strides=(1,),
                                   padding=[((K - 1) // 2, K // 2)],
                                   dimension_numbers=('NWC', 'WIO', 'NWC'),
                                   feature_group_count=u.shape[-1])
    return out + b.astype(u.dtype)


def ssd_scan(x, dt, A, Bm, Cm):
    b, T, h, p = x.shape
    g, n = Bm.shape[2], Bm.shape[3]
    r = h // g
    L = SSD_CHUNK
    c = T // L
    f32 = jnp.float32
    xs = x.astype(f32).reshape(b, c, L, g, r, p)
    dt = dt.astype(f32).reshape(b, c, L, g, r)
    Bm = Bm.astype(f32).reshape(b, c, L, g, n)
    Cm = Cm.astype(f32).reshape(b, c, L, g, n)
    acs = jnp.cumsum(dt * A.astype(f32).reshape(g, r), axis=2)
    xdt = xs * dt[..., None]
    seg = acs[:, :, :, None] - acs[:, :, None, :]
    tril = jnp.tril(jnp.ones((L, L), dtype=bool))[:, :, None, None]
    decay = jnp.exp(jnp.where(tril, seg, -jnp.inf))
    cb = jnp.einsum('bcign,bcjgn->bcijg', Cm, Bm)
    y_diag = jnp.einsum('bcijgr,bcjgrp->bcigrp', cb[..., None] * decay, xdt)
    decay_end = jnp.exp(acs[:, :, -1:] - acs)
    states = jnp.einsum('bcjgn,bcjgrp->bcgrpn', Bm, xdt * decay_end[..., None])
    chunk_decay = jnp.exp(acs[:, :, -1])

    def step(S, inp):
        dA, st = inp
        return dA[..., None, None] * S + st, S

    S0 = jnp.zeros((b, g, r, p, n), f32)
    _, S_prev = lax.scan(step, S0, (jnp.moveaxis(chunk_decay, 1, 0), jnp.moveaxis(states, 1, 0)))
    S_prev = jnp.moveaxis(S_prev, 0, 1)
    y_off = jnp.einsum('bcign,bcgrpn->bcigrp', Cm, S_prev) * jnp.exp(acs)[..., None]
    return (y_diag + y_off).reshape(b, T, h, p).astype(x.dtype)


def alibi_slopes(n_heads):
    return 2.0 ** (-8.0 * jnp.arange(1, n_heads + 1, dtype=jnp.float32) / n_heads)


def window_attention(q, k, v, sink):
    b, T, H, d = q.shape
    KV = k.shape[2]
    r = H // KV
    nb = T // BLOCK
    qb = q.reshape(b, nb, BLOCK, KV, r, d)
    pad = ((0, 0), (BLOCK, BLOCK), (0, 0), (0, 0))
    kp = jnp.pad(k, pad).reshape(b, nb + 2, BLOCK, KV, d)
    vp = jnp.pad(v, pad).reshape(b, nb + 2, BLOCK, KV, d)
    kw = jnp.concatenate([kp[:, :-2], kp[:, 1:-1], kp[:, 2:]], axis=2)
    vw = jnp.concatenate([vp[:, :-2], vp[:, 1:-1], vp[:, 2:]], axis=2)
    s = jnp.einsum('bnqgrd,bnkgd->bngrqk', qb, kw).astype(jnp.float32) * (d ** -0.5)
    blk = jnp.arange(nb)[:, None] * BLOCK
    qpos = blk + jnp.arange(BLOCK)[None, :]
    kpos = blk - BLOCK + jnp.arange(3 * BLOCK)[None, :]
    dist = jnp.abs(qpos[:, :, None] - kpos[:, None, :]).astype(jnp.float32)
    valid = (dist <= WINDOW) & ((kpos >= 0) & (kpos < T))[:, None, :]
    slopes = alibi_slopes(H).reshape(KV, r)
    logits = jnp.where(valid[:, None, None], s - slopes[None, :, :, None, None] * dist[:, None, None], -jnp.inf)
    sk = sink.astype(jnp.float32).reshape(KV, r)[:, :, None, None]
    m = jnp.maximum(jnp.max(logits, axis=-1, keepdims=True), sk)
    pr = jnp.exp(logits - m)
    probs = pr / (jnp.sum(pr, axis=-1, keepdims=True) + jnp.exp(sk - m))
    o = jnp.einsum('bngrqk,bnkgd->bnqgrd', probs.astype(v.dtype), vw)
    return o.reshape(b, T, H * d)


def layer(x, mem, norm_g, w_in, conv_w, conv_b, dt_bias, a_log, d_skip, ssd_norm_g,
          q_norm_g, k_norm_g, sink, mem_norm_g, w_mem_kv, mq_norm_g, mk_norm_g, w_out):
    b, T, _ = x.shape
    hn = rms_norm(x, norm_g)
    proj = hn @ w_in
    xbc, z_ssd, dt_raw, q, k, v, z_att, mq, z_mem = split_cols(proj, IN_SIZES)

    xbc = jax.nn.silu(centred_dwconv(xbc, conv_w, conv_b))
    xs, Bm, Cm = split_cols(xbc, (SSD_WIDTH, SSD_GROUPS * SSD_STATE, SSD_GROUPS * SSD_STATE))
    xs = xs.reshape(b, T, SSD_HEADS, SSD_HEAD_DIM)
    Bm = Bm.reshape(b, T, SSD_GROUPS, SSD_STATE)
    Cm = Cm.reshape(b, T, SSD_GROUPS, SSD_STATE)
    dt = jax.nn.softplus(dt_raw.astype(jnp.float32).reshape(b, T, 2, SSD_HEADS) + dt_bias.astype(jnp.float32))
    A = -jnp.exp(a_log.astype(jnp.float32))
    y_f = ssd_scan(xs, dt[:, :, 0], A[0], Bm, Cm)
    y_b = ssd_scan(xs[:, ::-1], dt[:, ::-1, 1], A[1], Bm[:, ::-1], Cm[:, ::-1])[:, ::-1]
    y = (y_f + y_b + d_skip[:, None] * xs).reshape(b, T, SSD_WIDTH)
    yg = (y * jax.nn.silu(z_ssd)).reshape(b, T, SSD_GROUPS, SSD_WIDTH // SSD_GROUPS)
    o_ssd = rms_norm(yg, ssd_norm_g.reshape(SSD_GROUPS, SSD_WIDTH // SSD_GROUPS)).reshape(b, T, SSD_WIDTH)

    q = rms_norm(q.reshape(b, T, ATT_HEADS, ATT_HEAD_DIM), q_norm_g)
    k = rms_norm(k.reshape(b, T, ATT_KV_HEADS, ATT_HEAD_DIM), k_norm_g)
    v = v.reshape(b, T, ATT_KV_HEADS, ATT_HEAD_DIM)
    o_att = window_attention(q, k, v, sink) * jax.nn.silu(z_att)

    memn = rms_norm(mem, mem_norm_g)
    mk, mv = split_cols(memn @ w_mem_kv, (MEM_WIDTH, MEM_WIDTH))
    M = mem.shape[1]
    mk = rms_norm(mk.reshape(b, M, MEM_HEADS, MEM_HEAD_DIM), mk_norm_g)
    mv = mv.reshape(b, M, MEM_HEADS, MEM_HEAD_DIM)
    mq = rms_norm(mq.reshape(b, T, MEM_HEADS, MEM_HEAD_DIM), mq_norm_g)
    sm = jnp.einsum('bthd,bmhd->bhtm', mq, mk).astype(jnp.float32) * (MEM_HEAD_DIM ** -0.5)
    pm = jax.nn.softmax(sm, axis=-1).astype(mv.dtype)
    o_mem = jnp.einsum('bhtm,bmhd->bthd', pm, mv).reshape(b, T, MEM_WIDTH) * jax.nn.silu(z_mem)

    return x + jnp.concatenate([o_ssd, o_att, o_mem], axis=-1) @ w_out


def trunk(x, mem, params):
    for l in range(DEPTH):
        x = layer(x, mem, *[p[l] for p in params])
    return x


def setup_inputs(seed: int = 0) -> dict:
    key = jax.random.key(seed)
    ks = jax.random.split(key, 24)
    f32 = jnp.float32

    def nrm(k, shape, scale):
        return jax.random.normal(k, shape, f32) * scale

    u = jax.random.uniform(ks[8], (DEPTH, 2, SSD_HEADS), f32)
    dt0 = jnp.exp(u * (math.log(DT_MAX) - math.log(DT_MIN)) + math.log(DT_MIN))
    return {
        'x_prompt': nrm(ks[0], (BATCH, SEQ, D_MODEL), 1.0),
        'x_sample': nrm(ks[1], (DEC_BATCH, DEC_SEQ, D_MODEL), 1.0),
        'mem_prompt': nrm(ks[2], (BATCH, N_MEM, D_MODEL), 1.0),
        'mem_sample': nrm(ks[3], (DEC_BATCH, N_MEM, D_MODEL), 1.0),
        'norm_g': 1.0 + nrm(ks[4], (DEPTH, D_MODEL), 0.02),
        'w_in': nrm(ks[5], (DEPTH, D_MODEL, D_IN), D_MODEL ** -0.5),
        'conv_w': nrm(ks[6], (DEPTH, D_CONV, CONV_CH), D_CONV ** -0.5),
        'conv_b': nrm(ks[7], (DEPTH, CONV_CH), 0.02),
        'dt_bias': dt0 + jnp.log(-jnp.expm1(-dt0)),
        'a_log': jnp.log(jax.random.uniform(ks[9], (DEPTH, 2, SSD_HEADS), f32, minval=1.0, maxval=16.0)),
        'd_skip': 1.0 + nrm(ks[10], (DEPTH, SSD_HEADS), 0.02),
        'ssd_norm_g': 1.0 + nrm(ks[11], (DEPTH, SSD_WIDTH), 0.02),
        'q_norm_g': 1.0 + nrm(ks[12], (DEPTH, ATT_HEAD_DIM), 0.02),
        'k_norm_g': 1.0 + nrm(ks[13], (DEPTH, ATT_HEAD_DIM), 0.02),
        'sink': nrm(ks[14], (DEPTH, ATT_HEADS), 0.5),
        'mem_norm_g': 1.0 + nrm(ks[15], (DEPTH, D_MODEL), 0.02),
        'w_mem_kv': nrm(ks[16], (DEPTH, D_MODEL, 2 * MEM_WIDTH), D_MODEL ** -0.5),
        'mq_norm_g': 1.0 + nrm(ks[17], (DEPTH, MEM_HEAD_DIM), 0.02),
        'mk_norm_g': 1.0 + nrm(ks[18], (DEPTH, MEM_HEAD_DIM), 0.02),
        'w_out': nrm(ks[19], (DEPTH, D_MIX, D_MODEL), D_MIX ** -0.5),
    }


def reference(x_prompt, x_sample, mem_prompt, mem_sample, norm_g, w_in, conv_w, conv_b, dt_bias,
              a_log, d_skip, ssd_norm_g, q_norm_g, k_norm_g, sink, mem_norm_g, w_mem_kv,
              mq_norm_g, mk_norm_g, w_out):
    params = (norm_g, w_in, conv_w, conv_b, dt_bias, a_log, d_skip, ssd_norm_g, q_norm_g, k_norm_g,
              sink, mem_norm_g, w_mem_kv, mq_norm_g, mk_norm_g, w_out)
    y_prompt = trunk(x_prompt, mem_prompt, params)
    y_sample = trunk(x_sample, mem_sample, params)
    return (y_prompt, y_sample)
```

```python
import numpy as np
from contextlib import ExitStack
import concourse.bass as bass
import concourse.mybir as mybir
from concourse.bass_utils import run_bass_kernel_spmd

F32 = mybir.dt.float32
BF16 = mybir.dt.bfloat16
AF = mybir.ActivationFunctionType
ALU = mybir.AluOpType
AX = mybir.AxisListType

ENGS = ("pe", "act", "dve", "pool", "sp")


class _Op:
    __slots__ = ("eng", "fn", "deps", "odeps", "is_dma", "has_dep", "token", "clock", "idx", "dsem",
                 "cost", "lat", "start", "finish", "nwait", "tready", "users")

    def __init__(self, eng, fn, is_dma, cost, lat):
        self.eng = eng
        self.fn = fn
        self.deps = set()
        self.odeps = set()
        self.is_dma = is_dma
        self.has_dep = False
        self.token = None
        self.clock = None
        self.dsem = None
        self.cost = max(float(cost), 1.0)
        self.lat = float(lat)
        self.start = None
        self.finish = None
        self.users = []


class Prog:
    def __init__(self, nc, n_dma_sems=12, window=300):
        self.nc = nc
        self.ops = []
        self.last_w = {}
        self.readers = {}
        self.n_dma_sems = n_dma_sems
        self.window = window
        self.use_prio = True
        self.xlat = 200.0
        self.slat = 60.0

    def op(self, eng, fn, reads=(), writes=(), dma=False, cost=100.0, lat=0.0):
        o = _Op(eng, fn, dma, cost, lat)
        o.idx = len(self.ops)
        ops = self.ops
        for k in reads:
            w = self.last_w.get(k)
            if w is not None:
                o.deps.add(w)
            if isinstance(k, str) and k.startswith("ps"):
                for r in self.readers.get(k, ()):
                    if ops[r].eng != eng:
                        o.deps.add(r)
        for k in writes:
            w = self.last_w.get(k)
            if w is not None:
                wo = ops[w]
                if eng == "pe" and wo.eng == "pe":
                    o.odeps.add(w)
                else:
                    o.deps.add(w)
            for r in self.readers.get(k, ()):
                o.deps.add(r)
        o.deps.discard(o.idx)
        o.odeps.discard(o.idx)
        o.odeps -= o.deps
        for k in writes:
            self.last_w[k] = o.idx
            self.readers[k] = []
        for k in reads:
            self.readers.setdefault(k, []).append(o.idx)
        for d in o.deps:
            ops[d].has_dep = True
        ops.append(o)
        return o

    def _schedule(self):
        ops = self.ops
        W = self.window
        n = len(ops)
        for o in ops:
            o.nwait = len(o.deps) + len(o.odeps)
            o.tready = 0.0
            for d in o.deps:
                ops[d].users.append(o.idx)
            for d in o.odeps:
                ops[d].users.append(o.idx)
        bl = [0.0] * n
        for o in reversed(ops):
            m = 0.0
            for u in o.users:
                if bl[u] > m:
                    m = bl[u]
            bl[o.idx] = m + o.cost + o.lat
        unsched = {e: [o.idx for o in ops if o.eng == e] for e in ENGS}
        head = {e: 0 for e in ENGS}
        done = [False] * n
        eng_free = {e: 0.0 for e in ENGS}
        sem_free = {q: [0.0] * self.n_dma_sems for q in ENGS}
        sem_last = {q: [None] * self.n_dma_sems for q in ENGS}
        order = {e: [] for e in ENGS}
        glob = []
        remaining = n
        cand = {e: None for e in ENGS}
        dirty = {e: True for e in ENGS}
        prio = self.use_prio

        def pick(e):
            lst = unsched[e]
            h = head[e]
            while h < len(lst) and done[lst[h]]:
                h += 1
            head[e] = h
            seen = 0
            i = h
            ef = eng_free[e]
            sf = min(sem_free[e])
            best_now = None
            best_later = None
            while i < len(lst) and seen < W:
                idx = lst[i]
                i += 1
                if done[idx]:
                    continue
                seen += 1
                o = ops[idx]
                if o.nwait:
                    continue
                st = o.tready if o.tready > ef else ef
                if o.is_dma and sf > st:
                    st = sf
                if st <= ef:
                    if not prio:
                        return (st, idx)
                    key = (-bl[idx], idx)
                    if best_now is None or key < best_now[0]:
                        best_now = (key, st, idx)
                else:
                    if best_later is None or (st, idx) < best_later:
                        best_later = (st, idx)
            if best_now is not None:
                return (best_now[1], best_now[2])
            return best_later

        while remaining:
            best = None
            for e in ENGS:
                if dirty[e]:
                    cand[e] = pick(e)
                    dirty[e] = False
                c = cand[e]
                if c is not None and (best is None or c < best[0]):
                    best = (c, e)
            assert best is not None, "scheduler stuck (dependency cycle?)"
            (st, idx), e = best
            o = ops[idx]
            o.start = st
            if o.is_dma:
                sl = min(range(self.n_dma_sems), key=lambda j: sem_free[e][j])
                o.dsem = sl
                p = sem_last[e][sl]
                if p is not None:
                    o.deps.add(p)
                eng_free[e] = st + o.cost
                o.finish = st + o.cost + o.lat
                sem_free[e][sl] = o.finish
                sem_last[e][sl] = idx
            else:
                eng_free[e] = st + o.cost
                o.finish = st + o.cost
            done[idx] = True
            remaining -= 1
            order[e].append(idx)
            glob.append(idx)
            dirty[e] = True
            for u in o.users:
                uo = ops[u]
                uo.nwait -= 1
                f_ = o.finish if idx in uo.odeps else o.finish + (self.xlat if uo.eng != e else self.slat)
                if f_ > uo.tready:
                    uo.tready = f_
                dirty[uo.eng] = True
        self.sim_time = max(o.finish for o in ops)
        return order, glob

    def emit(self):
        nc = self.nc
        ops = self.ops
        order, glob = self._schedule()
        with ExitStack() as es:
            esem = {e: es.enter_context(nc.semaphore(f"s_{e}")) for e in ENGS}
            dsem = {
                q: [es.enter_context(nc.semaphore(f"d_{q}{i}")) for i in range(self.n_dma_sems)]
                for q in ("sp", "pool", "act")
            }
            cnt = {e: 0 for e in ENGS}
            dcnt = {}
            for e in ENGS:
                for idx in order[e]:
                    o = ops[idx]
                    if o.is_dma:
                        key = (o.eng, o.dsem)
                        dcnt[key] = dcnt.get(key, 0) + 16
                        o.token = (key, dcnt[key])
                    elif o.has_dep:
                        cnt[e] += 1
                        o.token = ((e, None), cnt[e])
            known = {e: {} for e in ENGS}
            need = {}
            for idx in glob:
                o = ops[idx]
                kn = known[o.eng]
                best = {}
                for d in sorted(o.deps):
                    do = ops[d]
                    tk, tv = do.token
                    if kn.get(tk, 0) >= tv:
                        continue
                    if best.get(tk, 0) < tv:
                        best[tk] = tv
                    for k2, v2 in do.clock.items():
                        if kn.get(k2, 0) < v2:
                            kn[k2] = v2
                    if kn.get(tk, 0) < tv:
                        kn[tk] = tv
                need[idx] = list(best.items())
                if o.token is not None:
                    ck = dict(kn)
                    ck[o.token[0]] = o.token[1]
                    o.clock = ck

            def semh(key):
                return esem[key[0]] if key[1] is None else dsem[key[0]][key[1]]

            block = es.enter_context(nc.Block())

            def run_stream(ename, engobj):
                for idx in order[ename]:
                    o = ops[idx]
                    for tk, tv in need[idx]:
                        engobj.wait_ge(semh(tk), tv)
                    ins = o.fn(engobj)
                    if o.token is not None:
                        ins.then_inc(semh(o.token[0]), 16 if o.is_dma else 1)
                if ename in ("sp", "pool", "act"):
                    for (q, i), v in dcnt.items():
                        if q == ename:
                            engobj.wait_ge(dsem[q][i], v)

            @block.sync
            def _(e):
                run_stream("sp", e)

            @block.tensor
            def _(e):
                run_stream("pe", e)

            @block.scalar
            def _(e):
                run_stream("act", e)

            @block.vector
            def _(e):
                run_stream("dve", e)

            @block.gpsimd
            def _(e):
                run_stream("pool", e)


D = 2048
KC = 16
GW = 256
NMEM = 256
EPS = 1e-6
NFG = 6
NTG = 13
NOG = 8
NMG = 4
TG_Z0, TG_Q0, TG_MQ0, TG_KV = 0, 8, 10, 12
DEBUG_STOP = None


def build_program(segs, debug_taps=None):
    nc = bass.Bass("TRN2", target_bir_lowering=False)
    nseg = len(segs)
    dr = {}

    def din(name, shape, dt=F32):
        dr[name] = nc.dram_tensor(name, list(shape), dt, kind="ExternalInput").ap()
        return dr[name]

    xs = [din(f"x{i}", [T, D]) for i, (T, H) in enumerate(segs)]
    mems = [din(f"mem{i}", [NMEM, D]) for i in range(nseg)]
    w_feat = din("w_feat", [NFG, 128, KC, GW])
    w_tok = din("w_tok", [NTG, 128, KC, GW])
    w_dt = din("w_dt", [128, KC, 32])
    w_out = din("w_out", [NOG, 128, KC, GW])
    w_mem = din("w_mem", [NMG, 128, KC, GW])
    g_norm = din("g_norm", [128, KC])
    g_ssd = din("g_ssd", [128, 8])
    g_mem = din("g_mem", [128, KC])
    convw_d = din("convw", [128, 12, 5])
    convb_d = din("convb", [128, 12])
    dtb_d = din("dtb", [1, 32])
    alog_d = din("alog", [1, 32])
    dskip_d = din("dskip", [1, 16])
    gq_d = din("gq", [1, 64])
    gk_d = din("gk", [1, 64])
    sink_d = din("sink", [1, 8])
    gmq_d = din("gmq", [1, 128])
    gmk_d = din("gmk", [1, 128])
    cmat_d = din("cmat", [128, 5, 128])
    alibi_d = din("alibi", [128, 6, 512])
    ys = [nc.dram_tensor(f"y{i}", [H, D], F32, kind="ExternalOutput").ap() for i, (T, H) in enumerate(segs)]
    wb_feat = nc.dram_tensor("wb_feat", [NFG, 128, KC, GW], BF16, kind="Internal").ap()
    wb_tok = nc.dram_tensor("wb_tok", [NTG, 128, KC, GW], BF16, kind="Internal").ap()
    wb_out = nc.dram_tensor("wb_out", [NOG, 128, KC, GW], BF16, kind="Internal").ap()
    wb_mem = nc.dram_tensor("wb_mem", [NMG, 128, KC, GW], BF16, kind="Internal").ap()
    max_own_chunks = max(H // 128 for T, H in segs)
    sbst = nc.dram_tensor("sbst", [max_own_chunks, 128, 1024], BF16, kind="Internal").ap()
    hn_st = nc.dram_tensor("hn_st", [max_own_chunks + 1, 128, KC, 128], BF16, kind="Internal").ap()
    cv_st = nc.dram_tensor("cv_st", [max_own_chunks // 4, 128, 10, 512], BF16, kind="Internal").ap()

    with ExitStack() as es:
        def sb(name, shape, dt):
            return es.enter_context(nc.sbuf_tensor(name, list(shape), dt))

        def ps(name, shape, dt):
            return es.enter_context(nc.psum_tensor(name, list(shape), dt))

        wbuf = sb("wbuf", [128, 2, KC, GW], BF16)
        hnT = sb("hnT", [128, 8, KC, 128], BF16)
        xin = sb("xin", [128, 2, D], F32)
        u = sb("u", [128, 12, 516], BF16)
        utail = sb("utail", [128, 12, 2], BF16)
        utail4 = sb("utail4", [128, 10, 6], BF16)
        cv = sb("cv", [128, 12, 512], BF16)
        xtok = sb("xtok", [128, 1280], BF16)
        zs = sb("zs", [128, 4, D], BF16)
        qn_all = sb("qn_all", [128, 4, 512], BF16)
        mqn_all = sb("mqn_all", [128, 4, 512], BF16)
        kT = sb("kT", [64, 8, 2, 128], BF16)
        vaug = sb("vaug", [128, 8, 2, 65], BF16)
        qT = sb("qT", [64, 8, 128], BF16)
        mqT = sb("mqT", [128, 4, 128], BF16)
        PT = sb("PT", [128, 3, 512], BF16)
        PmT = sb("PmT", [128, 2, 4, 128], BF16)
        mkT = sb("mkT", [128, 4, NMEM], BF16)
        mvaug = sb("mvaug", [128, 2, 4, 129], BF16)
        raw = sb("raw", [128, 256], F32)
        sqb = sb("sqb", [128, 256], F32)
        nrm_bf = sb("nrm_bf", [128, 256], BF16)
        ssn = sb("ssn", [128, 8], F32)
        rstdn = sb("rstdn", [128, 8], F32)
        ssx = sb("ssx", [128, 1], F32)
        rstdx = sb("rstdx", [128, 1], F32)
        _uf = u[:].rearrange("p a b -> p (a b)")
        Xd = _uf[:, 0:2048].rearrange("p (h n) -> p h n", n=128)
        Ed = _uf[:, 2048:4096].rearrange("p (h n) -> p h n", n=128)
        Mf = _uf[:, 4096:6144].rearrange("p (h n) -> p h n", n=128)
        UALL = [("u", cb_) for cb_ in range(12)]
        Mb = sb("Mb", [128, 16, 128], BF16)
        CBm = sb("CBm", [128, 2, 2, 128], BF16)
        xdt = sb("xdt", [128, 2, 1024], BF16)
        xw = sb("xw", [128, 1024], BF16)
        S = sb("S", [128, 1024], F32)
        Sbf = sb("Sbf", [128, 1024], BF16)
        sbin = sb("sbin", [128, 1024], BF16)
        yasm = sb("yasm", [128, 1024], F32)
        ytmp = sb("ytmp", [128, 512], F32)
        otmp = sb("otmp", [128, 256], F32)
        den4 = sb("den4", [128, 4, 1], F32)
        oall = sb("oall", [128, D], BF16)
        hn = oall
        oT = sb("oT", [128, 2, KC, 128], BF16)
        Dw = sb("Dw", [128, 12, 5, 128], BF16)
        xsk = sb("xsk", [128, 1024], BF16)
        cmat_f = sb("cmat_f", [128, 5, 128], F32)
        cmat_b = sb("cmat_b", [128, 5, 128], BF16)
        onesf = sb("onesf", [128, 128], F32)
        alibi_b = sb("alibi_b", [128, 6, 512], BF16)
        wdt = sb("wdt", [128, KC, 32], BF16)
        gn = sb("gn", [128, KC], F32)
        gs = sb("gs", [128, KC], F32)
        gm = sb("gm", [128, KC], F32)
        convw = sb("convw_s", [128, 12, 5], F32)
        convb = sb("convb_s", [128, 12], F32)
        dtb_bc = sb("dtb_bc", [128, 32], F32)
        A_bc = sb("A_bc", [128, 32], F32)
        dskip_bc = sb("dskip_bc", [128, 16], F32)
        gq_bc = sb("gq_bc", [128, 64], F32)
        gk_bc = sb("gk_bc", [128, 64], F32)
        esink = sb("esink", [128, 8], F32)
        gmq_bc = sb("gmq_bc", [128, 128], F32)
        gmk_bc = sb("gmk_bc", [128, 128], F32)
        dtmp = sb("dtmp", [128, 4, 32], F32)
        dtv = sb("dtv", [128, 4, 32], F32)
        a4 = sb("a4", [128, 4, 32], F32)
        cum = sb("cum", [128, 4, 32], F32)
        eac = sb("eac", [128, 4, 32], F32)
        wgt = sb("wgt", [128, 4, 32], F32)
        dA = sb("dA", [128, 4, 32], F32)
        ss2 = sb("ss2", [128, 2], F32)
        fence_t = sb("fence_t", [128, 2], F32)
        rs2 = sb("rs2", [128, 2], F32)
        psA = [ps(f"psA{i}", [128, 512], F32) for i in range(4)]
        psT = ps("psT", [128, 2048], BF16)
        psY = ps("psY", [128, 1024], F32)

        _zf = zs[:].rearrange("p a b -> p (a b)")
        cv2 = _zf[:, 0:5120].rearrange("p (c n) -> p c n", n=512)
        ZSK = [("zs", i_) for i_ in range(4)]
        _qf = qn_all[:].rearrange("p a b -> p (a b)")
        xtok2 = _qf[:, 0:1280]
        QNK = [("qn_all", i_) for i_ in range(4)]
        _pf = PT[:].rearrange("p a b -> p (a b)")
        xw2 = _pf[:, 0:1024]
        PTK = [("PT", o_) for o_ in range(3)]
        _mf = mqn_all[:].rearrange("p a b -> p (a b)").bitcast(F32)
        MQK = [("mqn_all", i_) for i_ in range(4)]

        _dwf = Dw[:].rearrange("p a b c -> p (a b c)")
        wbuf3 = _dwf[:, 0:KC * GW].rearrange("p (k n) -> p k n", n=GW)
        DWK = [("Dw", cb_) for cb_ in range(7)]

        def wsl_ap(wsl):
            return wbuf3 if wsl == 2 else wbuf[:, wsl]

        def wk(wsl):
            return [("wbuf", wsl)] + (DWK if wsl == 2 else [])

        class BS:
            pass

        bs0 = BS()
        bs0.cv, bs0.cvk = cv, (lambda cb: [("cv", cb)])
        bs0.dtmp, bs0.dtv, bs0.a4, bs0.cum, bs0.eac, bs0.wgt, bs0.dA = dtmp, dtv, a4, cum, eac, wgt, dA
        bs0.dk = lambda n_: [n_]
        bs1 = BS()
        bs1.cv, bs1.cvk = cv2, (lambda cb: [("cv2", cb)] + ZSK)
        (bs1.dtmp, bs1.dtv, bs1.a4, bs1.cum, bs1.eac, bs1.wgt, bs1.dA) = [
            _mf[:, j_ * 128:(j_ + 1) * 128].rearrange("p (c n) -> p c n", n=32) for j_ in range(7)]
        bs1.dk = lambda n_: [n_ + "_2"] + MQK
        bsa = [BS(), BS()]
        for b_src, b_dst, flat_ in ((bs0, bsa[0], cv[:].rearrange("p a b -> p (a b)")), (bs1, bsa[1], _zf)):
            b_dst.__dict__.update(b_src.__dict__)
            b_dst.cv = flat_[:, 0:5140].rearrange("p (c n) -> p c n", n=514)
        CVALL = [("cv", cb_) for cb_ in range(12)]
        xs0 = BS()
        xs0.xtok, xs0.k0, xs0.k1, xs0.xw, xs0.kw = xtok, ["xtok0"], ["xtok1"], xw, ["xw"]
        xs1 = BS()
        xs1.xtok, xs1.k0, xs1.k1, xs1.xw, xs1.kw = xtok2, ["xtok20"] + QNK, ["xtok21"] + QNK, xw2, ["xw2"] + PTK

        identf = cmat_f[:, 0, :]
        Tf_f = cmat_f[:, 1, :]
        Tb_f = cmat_f[:, 2, :]
        identb = cmat_b[:, 0, :]
        Tf_b = cmat_b[:, 1, :]
        Tb_b = cmat_b[:, 2, :]
        Lf_b = cmat_b[:, 3, :]
        Lb_b = cmat_b[:, 4, :]

        P = Prog(nc)
        st = {"acc": 0, "wslot": 0, "xi": 0, "ev": 0, "nws": 2, "ua": []}

        def fsz(ap):
            n = 1
            for d_ in ap.shape[1:]:
                n *= d_
            return n

        def MM(out, lhsT, rhs, start, stop, r, w):
            n = fsz(rhs)
            c = max(n / 2.05, 60.0) + 13.0
            if rhs.dtype == F32:
                c *= 4
            P.op("pe", lambda e: e.matmul(out, lhsT=lhsT, rhs=rhs, start=start, stop=stop), r, w, cost=c)

        def TR(out, in_, r, w):
            P.op("pe", lambda e: e.transpose(out=out, in_=in_, identity=identb), list(r) + ["cmat_b"], w, cost=103.0)

        def ACT(out, in_, func, r, w, **kw):
            P.op("act", lambda e: e.activation(out=out, in_=in_, func=func, **kw), r, w, cost=224.0 + 0.84 * fsz(out))

        def ecost(eng, ap):
            n = fsz(ap)
            return (93.0 + 1.13 * n) if eng == "dve" else (200.0 + 1.73 * n)

        def TT(eng, out, in0, in1, op, r, w):
            P.op(eng, lambda e: e.tensor_tensor(out=out, in0=in0, in1=in1, op=op), r, w, cost=ecost(eng, out))

        def CP(eng, out, in_, r, w):
            if eng == "act":
                ACT(out, in_, AF.Copy, r, w)
            else:
                P.op(eng, lambda e: e.tensor_copy(out=out, in_=in_), r, w, cost=ecost(eng, out))

        def MSET(eng, ap, val, w):
            P.op(eng, lambda e: e.memset(ap, val), (), w, cost=0.5 * ecost(eng, ap))

        def DMA(q, out, in_, r, w):
            nb = fsz(out) * out.shape[0] * (2 if out.dtype == BF16 else 4)
            P.op(q, lambda e: e.dma_start(out=out, in_=in_), r, w, dma=True,
                 cost=(80.0 if q == "sp" else 900.0), lat=2000.0 + nb / 180.0)

        def next_acc():
            i = st["acc"]
            st["acc"] = (i + 1) % 4
            return psA[i], f"psA{i}"

        def evac_eng():
            st["ev"] ^= 1
            return "act" if st["ev"] else "dve"

        def bc(ap, shape, axis):
            return ap.unsqueeze(axis).broadcast_to(list(shape))

        def rsqrt_to(out_ap, in_ap, n, r, w):
            ACT(out_ap, in_ap, AF.Ln, r, w, scale=1.0 / n, bias=EPS)
            ACT(out_ap, out_ap, AF.Exp, w, w, scale=-0.5)

        DMA("sp", cmat_f[:], cmat_d, (), ["cmat_f"])
        CP("dve", cmat_b[:], cmat_f[:], ["cmat_f"], ["cmat_b"])
        MSET("pool", onesf[:], 1.0, ["onesf"])
        for h2 in range(2):
            for o3 in range(3):
                DMA("sp", xin[:, h2, o3 * 512:(o3 + 1) * 512], alibi_d[:, h2 * 3 + o3, :], (), [("xin", h2)])
            CP("dve", alibi_b[:, h2 * 3:(h2 + 1) * 3, :], xin[:, h2, 0:1536].rearrange("p (a n) -> p a n", n=512),
               [("xin", h2)], ["alibi_b"])
        DMA("sp", gn[:], g_norm, (), ["gn"])
        MSET("pool", gs[:], 1.0, ["gs"])
        DMA("sp", gs[:, 0:8], g_ssd, (), ["gs"])
        DMA("sp", gm[:], g_mem, (), ["gm"])
        DMA("sp", convw[:], convw_d, (), ["convw"])
        DMA("sp", convb[:], convb_d, (), ["convb"])
        for t_, d_, n_ in ((dtb_bc, dtb_d, "dtb_bc"), (A_bc, alog_d, "A_bc"), (dskip_bc, dskip_d, "dskip_bc"),
                           (gq_bc, gq_d, "gq_bc"), (gk_bc, gk_d, "gk_bc"), (esink, sink_d, "esink"),
                           (gmq_bc, gmq_d, "gmq_bc"), (gmk_bc, gmk_d, "gmk_bc")):
            DMA("sp", t_[:], d_.partition_broadcast(128), (), [n_])
        ACT(A_bc[:], A_bc[:], AF.Exp, ["A_bc"], ["A_bc"])
        P.op("dve", lambda e: e.tensor_scalar_mul(out=A_bc[:], in0=A_bc[:], scalar1=-1.0), ["A_bc"], ["A_bc"])
        ACT(esink[:], esink[:], AF.Exp, ["esink"], ["esink"])
        MSET("pool", vaug[:, :, :, 64:65], 1.0, [("vaug", s) for s in range(8)])
        MSET("pool", mvaug[:, :, :, 128:129], 1.0, ["mvaug"])

        engs3 = ["dve", "pool", "dve", "act"]

        def prep(src, dst, G, gcol, gname):
            for g in (range(G) if isinstance(G, int) else G):
                wsl = st["wslot"]
                st["wslot"] ^= 1
                for half in range(2):
                    DMA("sp", xin[:, half, :].rearrange("p (k n) -> p k n", n=GW), src[g, :, half * 8:(half + 1) * 8, :],
                        (), [("xin", half)])
                    eng = engs3[(2 * g + half) % 2 * 1]
                    eng = "dve" if half == 0 else "pool"
                    TT(eng, wsl_ap(wsl)[:, half * 8:(half + 1) * 8, :], xin[:, half, :].rearrange("p (k n) -> p k n", n=GW),
                       bc(gcol[:, half * 8:(half + 1) * 8], [128, 8, GW], 2), ALU.mult,
                       [("xin", half), gname], [*wk(wsl)])
                DMA("sp", dst[g], wsl_ap(wsl), [*wk(wsl)], [(dst.tensor.name, g)])

        prep(w_mem, wb_mem, NMG, gm, "gm")
        prep(w_feat, wb_feat, range(5), gn, "gn")

        _otf = oT[:].rearrange("p a k n -> p (a k n)").bitcast(F32)
        bg_in = _otf.rearrange("p (k n) -> p k n", n=GW)
        OTK = [("oT", a_, h_) for a_ in range(2) for h_ in range(2)]
        bg_out = yasm[:].bitcast(BF16).rearrange("p (k n) -> p k n", n=GW)

        def bg_prep_gen():
            jobs = [(w_feat, wb_feat, 5, gn, "gn")]
            jobs += [(w_tok, wb_tok, g_, gn, "gn") for g_ in range(NTG)]
            jobs += [(w_out, wb_out, g_, gs, "gs") for g_ in range(NOG)]
            n_ = 0
            for src, dst, g, gcol, gname in jobs:
                for half in range(2):
                    DMA("sp", bg_in, src[g, :, half * 8:(half + 1) * 8, :], (), OTK)
                    TT("dve" if n_ % 2 == 0 else "pool", bg_out, bg_in,
                       bc(gcol[:, half * 8:(half + 1) * 8], [128, 8, GW], 2), ALU.mult, OTK + [gname], [("yasm", 0), ("yasm", 1)])
                    DMA("sp", dst[g, :, half * 8:(half + 1) * 8, :], bg_out, [("yasm", 0), ("yasm", 1)], [(dst.tensor.name, g)])
                    n_ += 1
                    yield

        bg = bg_prep_gen()
        DMA("sp", xin[:, 0, 0:512].rearrange("p (k n) -> p k n", n=32), w_dt, (), [("xin", 0)])
        TT("dve", wdt[:], xin[:, 0, 0:512].rearrange("p (k n) -> p k n", n=32), bc(gn[:], [128, KC, 32], 2), ALU.mult,
           [("xin", 0), "gn"], ["wdt"])

        def load_w(dst, g):
            wsl = st["wslot"]
            st["wslot"] = (wsl + 1) % st["nws"]
            DMA("sp", wsl_ap(wsl), dst[g], [(dst.tensor.name, g)], [*wk(wsl)])
            return wsl

        def build_dw(cbs):
            for cb in cbs:
                for k in range(5):
                    P.op("dve", lambda e, cb=cb, k=k: e.tensor_scalar_mul(out=Dw[:, cb, k, :], in0=identf,
                                                                             scalar1=convw[:, cb, k:k + 1]),
                         ["cmat_f", "convw"], [("Dw", cb)], cost=180.0)

        ring = {}
        build_dw([10, 11])

        def norm_block(src_rows, slot):
            xi = st["xi"]
            st["xi"] ^= 1
            DMA("sp", xin[:, xi, :], src_rows, (), [("xin", xi)])
            MSET("pool", ssx[:], 0.0, ["ssx"])
            ACT(hn[:], xin[:, xi, :], AF.Square, [("xin", xi)], ["oall", "ssx"], accum_out=ssx[:])
            rsqrt_to(rstdx[:], ssx[:], D, ["ssx"], ["rstdx"])
            ACT(hn[:], xin[:, xi, :], AF.Identity, [("xin", xi), "rstdx"], ["oall"], scale=rstdx[:, 0:1])
            pv = psT[:].rearrange("p (k n) -> p k n", n=128)
            for k in range(KC):
                TR(pv[:, k, :], hn[:, k * 128:(k + 1) * 128], ["oall"], [f"psT{k // 8}"])
            CP("dve", hnT[:, slot, 0:8, :], pv[:, 0:8, :], ["psT0"], [("hnT", slot, 0)])
            CP("act", hnT[:, slot, 8:16, :], pv[:, 8:16, :], ["psT1"], [("hnT", slot, 1)])

        def ensure_hn(si, blk, store_upto=-1, load=False):
            slot = blk % 8
            if ring.get(slot) == (si, blk):
                return
            ring[slot] = (si, blk)
            if load:
                DMA("sp", hnT[:, slot], hn_st[blk], [("hn_st", blk)], [("hnT", slot, 0), ("hnT", slot, 1)])
                return
            norm_block(xs[si][blk * 128:(blk + 1) * 128, :], slot)
            if blk <= store_upto:
                DMA("pool", hn_st[blk], hnT[:, slot], [("hnT", slot, 0), ("hnT", slot, 1)], [("hn_st", blk)])

        def headnorm(acc, accn, nh, hd, gbc, gname, out_ap, out_keys):
            w_ = nh * hd
            CP("act", raw[:, 0:w_], acc[:, 0:w_], [accn], ["raw"])
            ACT(sqb[:, 0:w_], acc[:, 0:w_], AF.Square, [accn], ["sqb"])
            P.op("dve", lambda e: e.tensor_reduce(out=ssn[:, 0:nh], in_=sqb[:, 0:w_].rearrange("p (h d) -> p h d", d=hd),
                                                   axis=AX.X, op=ALU.add), ["sqb"], ["ssn"], cost=70.0 + 0.85 * w_)
            rsqrt_to(rstdn[:, 0:nh], ssn[:, 0:nh], hd, ["ssn"], ["rstdn"])
            r3 = raw[:, 0:w_].rearrange("p (h d) -> p h d", d=hd)
            TT("dve", r3, r3, bc(rstdn[:, 0:nh], [128, nh, hd], 2), ALU.mult, ["raw", "rstdn"], ["raw"])
            TT("pool", out_ap, r3, bc(gbc[:, 0:hd], [128, nh, hd], 1), ALU.mult, ["raw", gname], out_keys)

        def mem_kv(si):
            for mb in range(2):
                ring[6 + mb] = None
                norm_block(mems[si][mb * 128:(mb + 1) * 128, :], 6 + mb)
            for grp in range(NMG):
                wsl = load_w(wb_mem, grp)
                for mb in range(2):
                    acc, accn = next_acc()
                    for k in range(KC):
                        MM(acc[:, 0:GW], hnT[:, 6 + mb, k, :], wsl_ap(wsl)[:, k, :], k == 0, k == KC - 1,
                           [("hnT", 6 + mb, k // 8), *wk(wsl)], [accn])
                    if grp < 2:
                        headnorm(acc, accn, 2, 128, gmk_bc, "gmk_bc",
                                 nrm_bf[:, 0:256].rearrange("p (h d) -> p h d", d=128), ["nrm_bf"])
                        for hh in range(2):
                            TR(psT[:, 0:128], nrm_bf[:, hh * 128:(hh + 1) * 128], ["nrm_bf"], ["psT0"])
                            CP("dve", mkT[:, grp * 2 + hh, mb * 128:(mb + 1) * 128], psT[:, 0:128], ["psT0"], ["mkT"])
                    else:
                        g2 = grp - 2
                        CP("act", mvaug[:, mb, 2 * g2:2 * g2 + 2, 0:128], acc[:, 0:256].rearrange("p (h d) -> p h d", d=128),
                           [accn], ["mvaug"])

        def feat_inproj(si, b0, groups, asc, NB, first):
            ua = st["ua"]
            s0 = b0 % 8
            ahead = b0 + 4 if asc else b0 - 1
            has_ahead = 0 <= ahead < NB
            for grp in groups:
                wsl = load_w(wb_feat, grp)
                for j in range(2):
                    cb = 2 * grp + j
                    bh = u[:, cb, 0:2] if asc else u[:, cb, 514:516]
                    if first:
                        MSET("pool", bh, 0.0, [("u", cb)] + ua)
                    else:
                        CP("pool", bh, utail[:, cb, :], [("utail", cb)], [("u", cb)] + ua)
                    acc, accn = next_acc()
                    lw = wsl_ap(wsl)[:, :, j * 128:(j + 1) * 128]
                    for k in range(KC):
                        MM(acc[:].rearrange("p (s n) -> p s n", n=128), lw[:, k, :], hnT[:, s0:s0 + 4, k, :],
                           k == 0, k == KC - 1, [("hnT", s0 + i, k // 8) for i in range(4)] + [*wk(wsl)], [accn])
                    CP("act", u[:, cb, 2:514], acc[:], [accn], [("u", cb)] + ua)
                    ah = u[:, cb, 514:516] if asc else u[:, cb, 0:2]
                    if has_ahead:
                        sa = ahead % 8
                        acc2, acc2n = next_acc()
                        tsl = slice(0, 2) if asc else slice(126, 128)
                        for k in range(KC):
                            MM(acc2[:, 0:2], lw[:, k, :], hnT[:, sa, k, tsl], k == 0, k == KC - 1,
                               [("hnT", sa, k // 8), *wk(wsl)], [acc2n])
                        CP("dve", ah, acc2[:, 0:2], [acc2n], [("u", cb)] + ua)
                    else:
                        MSET("pool", ah, 0.0, [("u", cb)] + ua)
                    sv = u[:, cb, 512:514] if asc else u[:, cb, 2:4]
                    CP("pool", utail[:, cb, :], sv, [("u", cb)], [("utail", cb)] + ua)

        def conv(cbs, bs=None, a_layout=False):
            bs = bs or bs0
            for cb in cbs:
                acc, accn = next_acc()
                for k in range(5):
                    MM(acc[:], Dw[:, cb, k, :], u[:, cb, k:k + 512], k == 0, k == 4, [("Dw", cb), ("u", cb)], [accn] + st["ua"])
                dst = bs.cv[:, cb, 2:514] if a_layout else bs.cv[:, cb, :]
                ACT(dst, acc[:], AF.Silu, [accn, "convb"], bs.cvk(cb), bias=convb[:, cb:cb + 1])

        def feat_inproj_a(si, b0, first):
            s0 = b0 % 8
            for grp in range(5):
                wsl = load_w(wb_feat, grp)
                for j in range(2):
                    cb = 2 * grp + j
                    if first:
                        MSET("pool", u[:, cb, 512:516], 0.0, [("u", cb)])
                    else:
                        CP("pool", u[:, cb, 512:516], utail4[:, cb, 2:6], [("utail4", cb)], [("u", cb)])
                    acc, accn = next_acc()
                    lw = wsl_ap(wsl)[:, :, j * 128:(j + 1) * 128]
                    for k in range(KC):
                        MM(acc[:].rearrange("p (s n) -> p s n", n=128), lw[:, k, :], hnT[:, s0:s0 + 4, k, :],
                           k == 0, k == KC - 1, [("hnT", s0 + i, k // 8) for i in range(4)] + [*wk(wsl)], [accn])
                    CP("act", u[:, cb, 0:512], acc[:], [accn], [("u", cb)])
                    CP("pool", utail4[:, cb, 2:6], u[:, cb, 0:4], [("u", cb)], [("utail4", cb)])

        def dt_prep(si, b0, bs=None):
            bs = bs or bs0
            dk = bs.dk
            dtmp, dtv, a4, cum, eac, wgt, dA = bs.dtmp, bs.dtv, bs.a4, bs.cum, bs.eac, bs.wgt, bs.dA
            acc, accn = next_acc()
            pd = acc[:, 0:128].rearrange("p (c n) -> p c n", n=32)
            for i in range(4):
                sl = (b0 + i) % 8
                for k in range(KC):
                    MM(pd[:, i, :], hnT[:, sl, k, :], wdt[:, k, :], k == 0, k == KC - 1, [("hnT", sl, k // 8), "wdt"], [accn])
            TT("dve", dtmp[:], pd, bc(dtb_bc[:], [128, 4, 32], 1), ALU.add, [accn, "dtb_bc"], dk("dtmp"))
            ACT(dtmp[:], dtmp[:], AF.Exp, dk("dtmp"), dk("dtmp"))
            ACT(dtv[:], dtmp[:], AF.Ln, dk("dtmp"), dk("dtv"), bias=1.0)
            TT("dve", a4[:], dtv[:], bc(A_bc[:], [128, 4, 32], 1), ALU.mult, dk("dtv") + ["A_bc"], dk("a4"))
            acc, accn = next_acc()
            pc = acc[:, 0:384].rearrange("p (c j n) -> p c j n", j=3, n=32)
            for i in range(4):
                for j, lh in enumerate((Tf_f, Tb_f, onesf[:])):
                    MM(pc[:, i, j, :], lh, a4[:, i, :], True, True, ["cmat_f", "onesf"] + dk("a4"), [accn])
            CP("dve", cum[:, :, 0:16], pc[:, :, 0, 0:16], [accn], dk("cum"))
            CP("dve", cum[:, :, 16:32], pc[:, :, 1, 16:32], [accn], dk("cum"))
            ACT(eac[:], cum[:], AF.Exp, dk("cum"), dk("eac"))
            TT("dve", dtmp[:], pc[:, :, 2, :], cum[:], ALU.subtract, [accn] + dk("cum"), dk("dtmp"))
            ACT(dtmp[:], dtmp[:], AF.Exp, dk("dtmp"), dk("dtmp"))
            TT("dve", wgt[:], dtv[:], dtmp[:], ALU.mult, dk("dtv") + dk("dtmp"), dk("wgt"))
            ACT(dA[:], pc[:, :, 2, :], AF.Exp, [accn], dk("dA"))

        def chunk_transposes(c, ncb, bs=None, xs_=None):
            bs = bs or bs0
            xs_ = xs_ or xs0
            for cb in range(ncb):
                TR(psT[:, cb * 128:(cb + 1) * 128], bs.cv[:, cb, c * 128:(c + 1) * 128], bs.cvk(cb), [f"psT{cb // 8}"])
            CP("dve", xs_.xtok[:, 0:1024], psT[:, 0:1024], ["psT0"], xs_.k0)
            CP("act", xs_.xtok[:, 1024:1280], psT[:, 1024:1280], ["psT1"], xs_.k1)

        def state_update(c, d0, bs=None, xs_=None):
            bs = bs or bs0
            xs_ = xs_ or xs0
            xtok_, xw_ = xs_.xtok, xs_.xw
            x3 = xtok_[:, 0:1024].rearrange("p (h d) -> p h d", d=64)
            TT("pool", xw_[:].rearrange("p (h d) -> p h d", d=64), x3, bc(bs.wgt[:, c, d0:d0 + 16], [128, 16, 64], 2), ALU.mult,
               xs_.k0 + bs.dk("wgt"), xs_.kw)
            for g in range(2):
                acc, accn = next_acc()
                MM(acc[:], xtok_[:, 1024 + g * 128:1024 + (g + 1) * 128], xw_[:, g * 512:(g + 1) * 512], True, True,
                   xs_.k1 + xs_.kw, [accn])
                Sg = S[:, g * 512:(g + 1) * 512]
                S3 = Sg.rearrange("p (h d) -> p h d", d=64)
                TT("dve", S3, S3, bc(bs.dA[:, c, d0 + g * 8:d0 + g * 8 + 8], [128, 8, 64], 2), ALU.mult,
                   [("S", g)] + bs.dk("dA"), [("S", g)])
                TT("dve", Sg, Sg, acc[:], ALU.add, [("S", g), accn], [("S", g)])
                CP("act", Sbf[:, g * 512:(g + 1) * 512], Sg, [("S", g)], [("Sbf", g)])

        def sweep_a(si):
            T, H = segs[si]
            NB = T // 128
            own_chunks = H // 128
            MSET("pool", S[:], 0.0, [("S", 0), ("S", 1)])
            MSET("pool", Sbf[:], 0.0, [("Sbf", 0), ("Sbf", 1)])
            ntiles = NB // 4

            def cvkeys(bs):
                return [k_ for cb_ in range(10) for k_ in bs.cvk(cb_)]

            def chunk(ti, c):
                bs = bsa[ti % 2]
                ch = ti * 4 + c
                xs_ = (xs0, xs1)[c % 2]
                if ch < own_chunks:
                    DMA("pool", sbst[ch], Sbf[:], [("Sbf", 0), ("Sbf", 1)], [("sbst", ch)])
                if ch == 0:
                    return
                chunk_transposes(c, 10, bs, xs_)
                state_update(c, 16, bs, xs_)

            def finish_tile(tj):
                bsj = bsa[tj % 2]
                if tj * 4 < own_chunks:
                    DMA("pool", cv_st[tj], bsj.cv[:, :, 0:512], cvkeys(bsj), [("cv_st", tj)])
                chunk(tj, 0)

            for ti in range(ntiles - 1, -1, -1):
                b0 = ti * 4
                bs = bsa[ti % 2]
                for blk in range(b0 + 3, b0 - 1, -1):
                    ensure_hn(si, blk, store_upto=own_chunks)
                for _ in range(3):
                    next(bg, None)
                feat_inproj_a(si, b0, ti == ntiles - 1)
                conv(range(10), bs, a_layout=True)
                if ti + 1 < ntiles:
                    bsn = bsa[(ti + 1) % 2]
                    CP("dve", bsn.cv[:, :, 0:2], bs.cv[:, :, 512:514], cvkeys(bs), cvkeys(bsn))
                    finish_tile(ti + 1)
                dt_prep(si, b0, bs)
                for c in range(3, 0, -1):
                    chunk(ti, c)
            bs = bsa[0]
            MSET("pool", utail4[:, :, 0:2], 0.0, [("utail4", cb_) for cb_ in range(10)])
            for cb in range(10):
                acc, accn = next_acc()
                for k in range(5):
                    MM(acc[:, 0:2], Dw[:, cb, k, :], utail4[:, cb, k:k + 2], k == 0, k == 4, [("Dw", cb), ("utail4", cb)], [accn])
                ACT(bs.cv[:, cb, 0:2], acc[:, 0:2], AF.Silu, [accn, "convb"], bs.cvk(cb), bias=convb[:, cb:cb + 1])
            finish_tile(0)

        def tok_inproj(si, b0, NB):
            if DEBUG_STOP == "b_tok1":
                return
            for grp in range(2):
                wsl = load_w(wb_tok, TG_Q0 + grp)
                for i in range(4):
                    sl = (b0 + i) % 8
                    acc, accn = next_acc()
                    for k in range(KC):
                        MM(acc[:, 0:GW], hnT[:, sl, k, :], wsl_ap(wsl)[:, k, :], k == 0, k == KC - 1,
                           [("hnT", sl, k // 8), *wk(wsl)], [accn])
                    headnorm(acc, accn, 4, 64, gq_bc, "gq_bc",
                             qn_all[:, i, grp * 256:(grp + 1) * 256].rearrange("p (h d) -> p h d", d=64), [("qn_all", i)])
            if DEBUG_STOP == "b_tok2":
                return
            for grp in range(2):
                wsl = load_w(wb_tok, TG_MQ0 + grp)
                for i in range(4):
                    sl = (b0 + i) % 8
                    acc, accn = next_acc()
                    for k in range(KC):
                        MM(acc[:, 0:GW], hnT[:, sl, k, :], wsl_ap(wsl)[:, k, :], k == 0, k == KC - 1,
                           [("hnT", sl, k // 8), *wk(wsl)], [accn])
                    headnorm(acc, accn, 2, 128, gmq_bc, "gmq_bc",
                             mqn_all[:, i, grp * 256:(grp + 1) * 256].rearrange("p (h d) -> p h d", d=128), [("mqn_all", i)])
            if DEBUG_STOP == "b_tok3":
                return
            wsl = load_w(wb_tok, TG_KV)
            blks = [b for b in range(b0, min(b0 + 5, NB)) if kvring.get(b % 8) != (si, b)]
            for blk in blks:
                sl = blk % 8
                kvring[sl] = (si, blk)
                acc, accn = next_acc()
                for k in range(KC):
                    MM(acc[:, 0:GW], hnT[:, sl, k, :], wsl_ap(wsl)[:, k, :], k == 0, k == KC - 1,
                       [("hnT", sl, k // 8), *wk(wsl)], [accn])
                if DEBUG_STOP == "b_kv0":
                    return
                CP("dve", vaug[:, sl, :, 0:64], acc[:, 128:256].rearrange("p (g d) -> p g d", d=64), [accn], [("vaug", sl)])
                if DEBUG_STOP == "b_kv1":
                    return
                headnorm(acc, accn, 2, 64, gk_bc, "gk_bc", nrm_bf[:, 0:128].rearrange("p (h d) -> p h d", d=64), ["nrm_bf"])
                if DEBUG_STOP == "b_kv2":
                    return
                for g in range(2):
                    TR(psT[0:64, g * 128:(g + 1) * 128], nrm_bf[:, g * 64:(g + 1) * 64], ["nrm_bf"], ["psT0"])
                if DEBUG_STOP == "b_kv3":
                    return
                CP("dve", kT[:, sl, :, :], psT[0:64, 0:256].rearrange("p (g n) -> p g n", n=128), ["psT0"], [("kT", sl)])
                if DEBUG_STOP == "b_kv4":
                    return
                if DEBUG_STOP is not None and DEBUG_STOP.startswith("b_kvn") and blk - b0 + 1 == int(DEBUG_STOP[5:]):
                    return

        def gates_inproj(si, b0):
            for grp in range(8):
                wsl = load_w(wb_tok, grp)
                for i in range(4):
                    sl = (b0 + i) % 8
                    acc, accn = next_acc()
                    for k in range(KC):
                        MM(acc[:, 0:GW], hnT[:, sl, k, :], wsl_ap(wsl)[:, k, :], k == 0, k == KC - 1,
                           [("hnT", sl, k // 8), *wk(wsl)], [accn])
                    ACT(zs[:, i, grp * GW:(grp + 1) * GW], acc[:, 0:GW], AF.Silu, [accn], [("zs", i)])

        def attention(si, b0, i, NB):
            qb = b0 + i
            for h in range(8):
                TR(psT[0:64, h * 128:(h + 1) * 128], qn_all[:, i, h * 64:(h + 1) * 64], [("qn_all", i)], ["psT0"])
            CP("dve", qT[:], psT[0:64, 0:1024].rearrange("p (h n) -> p h n", n=128), ["psT0"], ["qT"])
            kbs = [kb for kb in (qb - 1, qb, qb + 1) if 0 <= kb < NB]
            for g in range(2):
                for kb in kbs:
                    o = kb - qb + 1
                    sl = kb % 8
                    acc, accn = next_acc()
                    MM(acc[:].rearrange("p (h n) -> p h n", n=128), kT[:, sl, g, :], qT[:, 4 * g:4 * g + 4, :], True, False,
                       [("kT", sl), "qT"], [accn])
                    MM(acc[:], identb, alibi_b[:, o * 2 + g, :], False, True, ["cmat_b", "alibi_b"], [accn])
                    ACT(PT[:, o, :], acc[:], AF.Exp, [accn], [("PT", o)], scale=0.125)
                acc, accn = next_acc()
                po = acc[:, 0:260].rearrange("p (h n) -> p h n", n=65)
                for r_ in range(4):
                    for n_, kb in enumerate(kbs):
                        o = kb - qb + 1
                        sl = kb % 8
                        MM(po[:, r_, :], PT[:, o, r_ * 128:(r_ + 1) * 128], vaug[:, sl, g, :], n_ == 0, n_ == len(kbs) - 1,
                           [("PT", o), ("vaug", sl)], [accn])
                TT("dve", den4[:], po[:, :, 64:65], esink[:, 4 * g:4 * g + 4].unsqueeze(2), ALU.add, [accn, "esink"], ["den4"])
                P.op("dve", lambda e: e.reciprocal(out=den4[:], in_=den4[:]), ["den4"], ["den4"])
                o3 = otmp[:].rearrange("p (h d) -> p h d", d=64)
                TT("dve", o3, po[:, :, 0:64], den4[:].broadcast_to([128, 4, 64]), ALU.mult, [accn, "den4"], ["otmp"])
                c0 = 1024 + g * 256
                TT("pool", oall[:, c0:c0 + 256], otmp[:], zs[:, i, c0:c0 + 256], ALU.mult, ["otmp", ("zs", i)], ["oall"])

        def mem_attention(i):
            for h in range(4):
                TR(psT[:, h * 128:(h + 1) * 128], mqn_all[:, i, h * 128:(h + 1) * 128], [("mqn_all", i)], ["psT0"])
            CP("act", mqT[:], psT[:, 0:512].rearrange("p (h n) -> p h n", n=128), ["psT0"], ["mqT"])
            for mc in range(2):
                acc, accn = next_acc()
                a3 = acc[:].rearrange("p (h n) -> p h n", n=128)
                for h in range(4):
                    MM(a3[:, h, :], mkT[:, h, mc * 128:(mc + 1) * 128], mqT[:, h, :], True, True, ["mkT", "mqT"], [accn])
                ACT(PmT[:, mc, :, :], a3, AF.Exp, [accn], ["PmT"], scale=128.0 ** -0.5)
            for hp in range(2):
                acc, accn = next_acc()
                po = acc[:, 0:258].rearrange("p (h n) -> p h n", n=129)
                for hh in range(2):
                    h = hp * 2 + hh
                    for mc in range(2):
                        MM(po[:, hh, :], PmT[:, mc, h, :], mvaug[:, mc, h, :], mc == 0, mc == 1, ["PmT", "mvaug"], [accn])
                P.op("dve", lambda e, po=po: e.reciprocal(out=den4[:, 0:2, :], in_=po[:, :, 128:129]), [accn], ["den4"])
                o3 = otmp[:].rearrange("p (h d) -> p h d", d=128)
                TT("dve", o3, po[:, :, 0:128], den4[:, 0:2, :].broadcast_to([128, 2, 128]), ALU.mult, [accn, "den4"], ["otmp"])
                c0 = 1536 + hp * 256
                TT("pool", oall[:, c0:c0 + 256], otmp[:], zs[:, i, c0:c0 + 256], ALU.mult, ["otmp", ("zs", i)], ["oall"])

        def ssd_chunk(si, b0, c, before_gate=None):
            ch = b0 + c
            DMA("sp", sbin[:], sbst[ch], [("sbst", ch)], ["sbin"])
            chunk_transposes(c, 10)
            x3 = xtok[:, 0:1024].rearrange("p (h d) -> p h d", d=64)
            for d in range(2):
                TT("pool", xdt[:, d, :].rearrange("p (h d) -> p h d", d=64), x3,
                   bc(dtv[:, c, d * 16:(d + 1) * 16], [128, 16, 64], 2), ALU.mult, ["xtok0", "dtv"], [("xdt", d)])
            TT("pool", xsk[:].rearrange("p (h d) -> p h d", d=64), x3, bc(dskip_bc[:], [128, 16, 64], 2), ALU.mult,
               ["xtok0", "dskip_bc"], ["xsk"])
            acc, accn = next_acc()
            cbp = acc[:, 0:256].rearrange("p (g n) -> p g n", n=128)
            for g in range(2):
                MM(cbp[:, g, :], cv[:, 8 + g, c * 128:(c + 1) * 128], cv[:, 10 + g, c * 128:(c + 1) * 128], True, True,
                   [("cv", 8 + g), ("cv", 10 + g)], [accn])
            TT("dve", CBm[:, 0, :, :], cbp, bc(Tf_b, [128, 2, 128], 1), ALU.mult, [accn, "cmat_b"], ["CBm"])
            TT("dve", CBm[:, 1, :, :], cbp, bc(Tb_b, [128, 2, 128], 1), ALU.mult, [accn, "cmat_b"], ["CBm"])
            for d, (Md, Mn, Ud, Ld) in enumerate(((Mf, "Mf", Tf_b, Lf_b), (Mb, "Mb", Tb_b, Lb_b))):
                ua = []
                for hq in range(4):
                    hs4 = slice(4 * hq, 4 * hq + 4)
                    TT("pool" if (hq + d) % 2 == 0 else "dve", Xd[:, hs4, :],
                       bc(a4[:, c, d * 16 + 4 * hq:d * 16 + 4 * hq + 4], [128, 4, 128], 2), bc(Ud, [128, 4, 128], 1), ALU.mult,
                       ["a4", "cmat_b", "UA"], [("Xd", hq)])
                    acc, accn = next_acc()
                    MM(acc[:].rearrange("p (h n) -> p h n", n=128), Ld, Xd[:, hs4, :], True, True,
                       ["cmat_b", ("Xd", hq)], [accn])
                    ACT(Ed[:, hs4, :], acc[:].rearrange("p (h n) -> p h n", n=128), AF.Exp, [accn, "UA"], [("Ed", hq)])
                    TT("dve", Md[:, hs4, :], Ed[:, hs4, :], bc(CBm[:, d, hq // 2, :], [128, 4, 128], 1), ALU.mult,
                       [("Ed", hq), "CBm", "UA"], [(Mn, hq)])
            for h in range(16):
                hs = slice(h * 64, (h + 1) * 64)
                yk = [f"psY{h // 8}"]
                MM(psY[:, hs], Mf[:, h, :], xdt[:, 0, hs], True, False, [("Mf", h // 4), ("xdt", 0)], yk)
                MM(psY[:, hs], Mb[:, h, :], xdt[:, 1, hs], False, False, [("Mb", h // 4), ("xdt", 1)], yk)
                MM(psY[:, hs], identb, xsk[:, hs], False, True, ["cmat_b", "xsk"], yk)
            MSET("pool", ss2[:], 0.0, [("ss2", 0), ("ss2", 1)])
            for g in range(2):
                gs_ = slice(g * 512, (g + 1) * 512)
                yk_ = [("yasm", g)]
                accf, accfn = next_acc()
                MM(accf[:], cv[:, 10 + g, c * 128:(c + 1) * 128], Sbf[:, gs_], True, True, [("cv", 10 + g), ("Sbf", g)], [accfn])
                accb, accbn = next_acc()
                MM(accb[:], cv[:, 10 + g, c * 128:(c + 1) * 128], sbin[:, gs_], True, True, [("cv", 10 + g), "sbin"], [accbn])
                y3 = yasm[:, gs_].rearrange("p (h d) -> p h d", d=64)
                t3 = ytmp[:].rearrange("p (h d) -> p h d", d=64)
                TT("dve", y3, accf[:].rearrange("p (h d) -> p h d", d=64), bc(eac[:, c, g * 8:g * 8 + 8], [128, 8, 64], 2), ALU.mult,
                   [accfn, "eac"], yk_)
                TT("dve", t3, accb[:].rearrange("p (h d) -> p h d", d=64), bc(eac[:, c, 16 + g * 8:16 + g * 8 + 8], [128, 8, 64], 2),
                   ALU.mult, [accbn, "eac"], ["ytmp"])
                TT("pool", yasm[:, gs_], yasm[:, gs_], ytmp[:], ALU.add, yk_ + ["ytmp"], yk_)
                TT("dve", yasm[:, gs_], yasm[:, gs_], psY[:, gs_], ALU.add, yk_ + [f"psY{g}"], yk_)
            if before_gate is not None:
                before_gate()
            for g in range(2):
                gs_ = slice(g * 512, (g + 1) * 512)
                yk_ = [("yasm", g)]
                TT("pool", yasm[:, gs_], yasm[:, gs_], zs[:, c, gs_], ALU.mult, yk_ + [("zs", c)], yk_)
                ACT(ytmp[:], yasm[:, gs_], AF.Square, yk_, ["ytmp", ("ss2", g)], accum_out=ss2[:, g:g + 1])
                rsqrt_to(rs2[:, g:g + 1], ss2[:, g:g + 1], 512, [("ss2", g)], [("rs2", g)])
                ACT(oall[:, gs_], yasm[:, gs_], AF.Identity, yk_ + [("rs2", g)], ["oall"], scale=rs2[:, g:g + 1])
            state_update(c, 0)

        def tail(si, b0, half):
            pv = psT[:].rearrange("p (k n) -> p k n", n=128)
            for ii in range(2):
                blk = b0 + 2 * half + ii
                DMA("sp", xin[:, ii, :], xs[si][blk * 128:(blk + 1) * 128, :], (), [("xin", ii)])
            for grp in range(NOG):
                wsl = load_w(wb_out, grp)
                for ii in range(2):
                    acc, accn = next_acc()
                    for k in range(KC):
                        MM(acc[:, 0:GW], oT[:, ii, k, :], wsl_ap(wsl)[:, k, :], k == 0, k == KC - 1,
                           [("oT", ii, k // 8), *wk(wsl)], [accn])
                    xsl = xin[:, ii, grp * GW:(grp + 1) * GW]
                    TT("dve", xsl, acc[:, 0:GW], xsl, ALU.add, [accn, ("xin", ii)], [("xin", ii)])
            for ii in range(2):
                blk = b0 + 2 * half + ii
                DMA("pool", ys[si][blk * 128:(blk + 1) * 128, :], xin[:, ii, :], [("xin", ii)], [("y", si, blk)])

        def oall_to_oT(ii):
            pv = psT[:].rearrange("p (k n) -> p k n", n=128)
            for k in range(KC):
                TR(pv[:, k, :], oall[:, k * 128:(k + 1) * 128], ["oall"], [f"psT{k // 8}"])
            CP("dve", oT[:, ii, 0:8, :], pv[:, 0:8, :], ["psT0"], [("oT", ii, 0)])
            CP("act", oT[:, ii, 8:16, :], pv[:, 8:16, :], ["psT1"], [("oT", ii, 1)])

        kvring = {}

        def sweep_b(si):
            T, H = segs[si]
            NB = T // 128
            MSET("pool", S[:], 0.0, [("S", 0), ("S", 1)])
            MSET("pool", Sbf[:], 0.0, [("Sbf", 0), ("Sbf", 1)])
            ntiles = (H // 128) // 4
            for ti in range(ntiles):
                b0 = ti * 4
                for blk in range(b0, min(b0 + 4, NB - 1) + 1):
                    ensure_hn(si, blk, load=True)
                DMA("sp", cv[:, 0:10, :], cv_st[ti], [("cv_st", ti)], [("cv", cb_) for cb_ in range(10)])
                feat_inproj(si, b0, [5], True, NB, ti == 0)
                conv([10, 11])
                if DEBUG_STOP == "b_feat":
                    return
                tok_inproj(si, b0, NB)
                dt_prep(si, b0)
                if DEBUG_STOP in ("b_tok", "b_tok1", "b_tok2", "b_tok3", "b_kv0", "b_kv1", "b_kv2", "b_kv3", "b_kv4") or (DEBUG_STOP or "").startswith("b_kvn"):
                    return
                for half in range(2):
                    for ii in range(2):
                        i = 2 * half + ii
                        if i == 0:
                            ssd_chunk(si, b0, i, lambda: gates_inproj(si, b0))
                            attention(si, b0, i, NB)
                            mem_attention(i)
                        else:
                            attention(si, b0, i, NB)
                            mem_attention(i)
                            ssd_chunk(si, b0, i)
                        oall_to_oT(ii)
                    tail(si, b0, half)
                    if DEBUG_STOP == "b_tail":
                        return

        for si in range(nseg):
            if DEBUG_STOP in ("setup", "prep"):
                break
            ring.clear()
            kvring.clear()
            mem_kv(si)
            ring.clear()
            if DEBUG_STOP == "mem":
                break
            st["nws"] = 2
            st["wslot"] = st["wslot"] % 2
            build_dw(range(10))
            st["ua"] = []
            MSET("pool", fence_t[:], 0.0, ["fence_t", "UA"] + UALL + CVALL)
            sweep_a(si)
            if DEBUG_STOP == "sweepa":
                break
            for _ in bg:
                pass
            st["nws"] = 3
            st["ua"] = ["UA"]
            MSET("pool", fence_t[:], 0.0, ["fence_t", "UA"] + UALL + CVALL)
            sweep_b(si)
            st["nws"] = 2
            st["wslot"] = st["wslot"] % 2

        P.emit()
    return nc


def _grp(W):
    C = W.shape[1]
    return np.ascontiguousarray(W.reshape(KC, 128, C // GW, GW).transpose(2, 1, 0, 3))


def _consts():
    i = np.arange(128)
    ident = np.eye(128, dtype=np.float32)
    Tf = (i[:, None] <= i[None, :]).astype(np.float32)
    Tb = (i[:, None] >= i[None, :]).astype(np.float32)
    Lf = (i[:, None] > i[None, :]).astype(np.float32)
    Lb = (i[:, None] < i[None, :]).astype(np.float32)
    cmat = np.ascontiguousarray(np.stack([ident, Tf, Tb, Lf, Lb], 1))
    al = np.zeros((128, 6, 512), np.float32)
    s_ = i[:, None]
    t_ = i[None, :]
    for o in range(3):
        dist = np.abs(t_ - s_ - 128 * (o - 1)).astype(np.float32)
        for g in range(2):
            for r in range(4):
                h = 4 * g + r
                slope = 2.0 ** (-(h + 1))
                v = np.where(dist <= 128, -8.0 * slope * dist, -240000.0)
                al[:, o * 2 + g, r * 128:(r + 1) * 128] = v
    return cmat, al


def _core_params(p, rev):
    w_in = p["w_in"]
    xbc, zssd, dtc, q, k, v, zatt, mq, zmem = np.split(w_in, np.cumsum([1536, 1024, 32, 512, 128, 128, 512, 512])[:], axis=1)
    dt_f, dt_b = dtc[:, 0:16], dtc[:, 16:32]
    dtb = p["dt_bias"].reshape(2, 16)
    alog = p["a_log"].reshape(2, 16)
    convw = p["conv_w"]
    if rev:
        dt_f, dt_b = dt_b, dt_f
        dtb = dtb[::-1]
        alog = alog[::-1]
        convw = convw[::-1]
    w_tok = np.concatenate([zssd, zatt, zmem, q, mq, k, v], axis=1)
    cmat, al = _consts()
    f = np.float32
    out = {
        "w_feat": _grp(xbc),
        "w_tok": _grp(w_tok),
        "w_dt": np.ascontiguousarray(np.concatenate([dt_f, dt_b], 1).reshape(KC, 128, 32).transpose(1, 0, 2)),
        "w_out": _grp(p["w_out"]),
        "w_mem": _grp(p["w_mem_kv"]),
        "g_norm": np.ascontiguousarray(p["norm_g"].reshape(KC, 128).T),
        "g_ssd": np.ascontiguousarray(p["ssd_norm_g"].reshape(8, 128).T),
        "g_mem": np.ascontiguousarray(p["mem_norm_g"].reshape(KC, 128).T),
        "convw": np.ascontiguousarray(convw.reshape(5, 12, 128).transpose(2, 1, 0)),
        "convb": np.ascontiguousarray(p["conv_b"].reshape(12, 128).T),
        "dtb": np.ascontiguousarray(dtb.reshape(1, 32)),
        "alog": np.ascontiguousarray(alog.reshape(1, 32)),
        "dskip": p["d_skip"].reshape(1, 16),
        "gq": p["q_norm_g"].reshape(1, 64),
        "gk": p["k_norm_g"].reshape(1, 64),
        "sink": p["sink"].reshape(1, 8),
        "gmq": p["mq_norm_g"].reshape(1, 128),
        "gmk": p["mk_norm_g"].reshape(1, 128),
        "cmat": cmat,
        "alibi": al,
    }
    return {k_: np.ascontiguousarray(v_, dtype=f) for k_, v_ in out.items()}


_PARAM_NAMES = ("norm_g", "w_in", "conv_w", "conv_b", "dt_bias", "a_log", "d_skip", "ssd_norm_g", "q_norm_g",
                "k_norm_g", "sink", "mem_norm_g", "w_mem_kv", "mq_norm_g", "mk_norm_g", "w_out")


def run_layer(seq_groups, params, n_pairs):
    p0 = {k: np.asarray(params[k], np.float32)[0] for k in _PARAM_NAMES}
    segs = [(x.shape[1], x.shape[1] // 2) for x, m in seq_groups]
    nc = build_program(segs)
    cp = [_core_params(p0, False), _core_params(p0, True)]
    in_maps = []
    for c in range(2 * n_pairs):
        kk, par = c // 2, c % 2
        m = dict(cp[par])
        for i, (x, mem) in enumerate(seq_groups):
            xx = np.asarray(x[kk], np.float32)
            m[f"x{i}"] = np.ascontiguousarray(xx[::-1] if par else xx)
            m[f"mem{i}"] = np.ascontiguousarray(np.asarray(mem[kk], np.float32))
        in_maps.append(m)
    res = run_bass_kernel_spmd(nc, in_maps, core_ids=list(range(2 * n_pairs)))
    outs = []
    for i, (x, mem) in enumerate(seq_groups):
        T = x.shape[1]
        H = T // 2
        y = np.empty((n_pairs, T, D), np.float32)
        for c in range(2 * n_pairs):
            kk, par = c // 2, c % 2
            yc = res.results[c][f"y{i}"]
            if par:
                y[kk, H:] = yc[::-1]
            else:
                y[kk, :H] = yc
        outs.append(y)
    return outs


def kernel(x_prompt, x_sample, mem_prompt, mem_sample, **params):
    outs = run_layer([(np.asarray(x_sample), np.asarray(mem_sample)), (np.asarray(x_prompt), np.asarray(mem_prompt))],
                     params, 4)
    return (outs[1], outs[0])
```

```python
import numpy as np
from contextlib import ExitStack
import concourse.bass as bass
import concourse.mybir as mybir
from concourse.bass_utils import run_bass_kernel_spmd

F32 = mybir.dt.float32
BF16 = mybir.dt.bfloat16
AF = mybir.ActivationFunctionType
ALU = mybir.AluOpType
AX = mybir.AxisListType

ENGS = ("pe", "act", "dve", "pool", "sp")


class _Op:
    __slots__ = ("eng", "fn", "deps", "odeps", "is_dma", "has_dep", "token", "clock", "idx", "dsem",
                 "cost", "lat", "start", "finish", "nwait", "tready", "users")

    def __init__(self, eng, fn, is_dma, cost, lat):
        self.eng = eng
        self.fn = fn
        self.deps = set()
        self.odeps = set()
        self.is_dma = is_dma
        self.has_dep = False
        self.token = None
        self.clock = None
        self.dsem = None
        self.cost = max(float(cost), 1.0)
        self.lat = float(lat)
        self.start = None
        self.finish = None
        self.users = []


class Prog:
    def __init__(self, nc, n_dma_sems=12, window=300):
        self.nc = nc
        self.ops = []
        self.last_w = {}
        self.readers = {}
        self.n_dma_sems = n_dma_sems
        self.window = window
        self.use_prio = True
        self.xlat = 200.0
        self.slat = 60.0

    def op(self, eng, fn, reads=(), writes=(), dma=False, cost=100.0, lat=0.0):
        o = _Op(eng, fn, dma, cost, lat)
        o.idx = len(self.ops)
        ops = self.ops
        for k in reads:
            w = self.last_w.get(k)
            if w is not None:
                o.deps.add(w)
            if isinstance(k, str) and k.startswith("ps"):
                for r in self.readers.get(k, ()):
                    if ops[r].eng != eng:
                        o.deps.add(r)
        for k in writes:
            w = self.last_w.get(k)
            if w is not None:
                wo = ops[w]
                if eng == "pe" and wo.eng == "pe":
                    o.odeps.add(w)
                else:
                    o.deps.add(w)
            for r in self.readers.get(k, ()):
                o.deps.add(r)
        o.deps.discard(o.idx)
        o.odeps.discard(o.idx)
        o.odeps -= o.deps
        for k in writes:
            self.last_w[k] = o.idx
            self.readers[k] = []
        for k in reads:
            self.readers.setdefault(k, []).append(o.idx)
        for d in o.deps:
            ops[d].has_dep = True
        ops.append(o)
        return o

    def _schedule(self):
        ops = self.ops
        W = self.window
        n = len(ops)
        for o in ops:
            o.nwait = len(o.deps) + len(o.odeps)
            o.tready = 0.0
            for d in o.deps:
                ops[d].users.append(o.idx)
            for d in o.odeps:
                ops[d].users.append(o.idx)
        bl = [0.0] * n
        for o in reversed(ops):
            m = 0.0
            for u in o.users:
                if bl[u] > m:
                    m = bl[u]
            bl[o.idx] = m + o.cost + o.lat
        unsched = {e: [o.idx for o in ops if o.eng == e] for e in ENGS}
        head = {e: 0 for e in ENGS}
        done = [False] * n
        eng_free = {e: 0.0 for e in ENGS}
        sem_free = {q: [0.0] * self.n_dma_sems for q in ENGS}
        sem_last = {q: [None] * self.n_dma_sems for q in ENGS}
        order = {e: [] for e in ENGS}
        glob = []
        remaining = n
        cand = {e: None for e in ENGS}
        dirty = {e: True for e in ENGS}
        prio = self.use_prio

        def pick(e):
            lst = unsched[e]
            h = head[e]
            while h < len(lst) and done[lst[h]]:
                h += 1
            head[e] = h
            seen = 0
            i = h
            ef = eng_free[e]
            sf = min(sem_free[e])
            best_now = None
            best_later = None
            while i < len(lst) and seen < W:
                idx = lst[i]
                i += 1
                if done[idx]:
                    continue
                seen += 1
                o = ops[idx]
                if o.nwait:
                    continue
                st = o.tready if o.tready > ef else ef
                if o.is_dma and sf > st:
                    st = sf
                if st <= ef:
                    if not prio:
                        return (st, idx)
                    key = (-bl[idx], idx)
                    if best_now is None or key < best_now[0]:
                        best_now = (key, st, idx)
                else:
                    if best_later is None or (st, idx) < best_later:
                        best_later = (st, idx)
            if best_now is not None:
                return (best_now[1], best_now[2])
            return best_later

        while remaining:
            best = None
            for e in ENGS:
                if dirty[e]:
                    cand[e] = pick(e)
                    dirty[e] = False
                c = cand[e]
                if c is not None and (best is None or c < best[0]):
                    best = (c, e)
            assert best is not None, "scheduler stuck (dependency cycle?)"
            (st, idx), e = best
            o = ops[idx]
            o.start = st
            if o.is_dma:
                sl = min(range(self.n_dma_sems), key=lambda j: sem_free[e][j])
                o.dsem = sl
                p = sem_last[e][sl]
                if p is not None:
                    o.deps.add(p)
                eng_free[e] = st + o.cost
                o.finish = st + o.cost + o.lat
                sem_free[e][sl] = o.finish
                sem_last[e][sl] = idx
            else:
                eng_free[e] = st + o.cost
                o.finish = st + o.cost
            done[idx] = True
            remaining -= 1
            order[e].append(idx)
            glob.append(idx)
            dirty[e] = True
            for u in o.users:
                uo = ops[u]
                uo.nwait -= 1
                f_ = o.finish if idx in uo.odeps else o.finish + (self.xlat if uo.eng != e else self.slat)
                if f_ > uo.tready:
                    uo.tready = f_
                dirty[uo.eng] = True
        self.sim_time = max(o.finish for o in ops)
        return order, glob

    def emit(self):
        nc = self.nc
        ops = self.ops
        order, glob = self._schedule()
        with ExitStack() as es:
            esem = {e: es.enter_context(nc.semaphore(f"s_{e}")) for e in ENGS}
            dsem = {
                q: [es.enter_context(nc.semaphore(f"d_{q}{i}")) for i in range(self.n_dma_sems)]
                for q in ("sp", "pool", "act")
            }
            cnt = {e: 0 for e in ENGS}
            dcnt = {}
            for e in ENGS:
                for idx in order[e]:
                    o = ops[idx]
                    if o.is_dma:
                        key = (o.eng, o.dsem)
                        dcnt[key] = dcnt.get(key, 0) + 16
                        o.token = (key, dcnt[key])
                    elif o.has_dep:
                        cnt[e] += 1
                        o.token = ((e, None), cnt[e])
            known = {e: {} for e in ENGS}
            need = {}
            for idx in glob:
                o = ops[idx]
                kn = known[o.eng]
                best = {}
                for d in sorted(o.deps):
                    do = ops[d]
                    tk, tv = do.token
                    if kn.get(tk, 0) >= tv:
                        continue
                    if best.get(tk, 0) < tv:
                        best[tk] = tv
                    for k2, v2 in do.clock.items():
                        if kn.get(k2, 0) < v2:
                            kn[k2] = v2
                    if kn.get(tk, 0) < tv:
                        kn[tk] = tv
                need[idx] = list(best.items())
                if o.token is not None:
                    ck = dict(kn)
                    ck[o.token[0]] = o.token[1]
                    o.clock = ck

            def semh(key):
                return esem[key[0]] if key[1] is None else dsem[key[0]][key[1]]

            block = es.enter_context(nc.Block())

            def run_stream(ename, engobj):
                for idx in order[ename]:
                    o = ops[idx]
                    for tk, tv in need[idx]:
                        engobj.wait_ge(semh(tk), tv)
                    ins = o.fn(engobj)
                    if o.token is not None:
                        ins.then_inc(semh(o.token[0]), 16 if o.is_dma else 1)
                if ename in ("sp", "pool", "act"):
                    for (q, i), v in dcnt.items():
                        if q == ename:
                            engobj.wait_ge(dsem[q][i], v)

            @block.sync
            def _(e):
                run_stream("sp", e)

            @block.tensor
            def _(e):
                run_stream("pe", e)

            @block.scalar
            def _(e):
                run_stream("act", e)

            @block.vector
            def _(e):
                run_stream("dve", e)

            @block.gpsimd
            def _(e):
                run_stream("pool", e)


D = 2048
KC = 16
GW = 256
NMEM = 256
EPS = 1e-6
NFG = 6
NTG = 13
NOG = 8
NMG = 4
TG_Z0, TG_Q0, TG_MQ0, TG_KV = 0, 8, 10, 12
DEBUG_STOP = None


def build_program(segs, debug_taps=None):
    nc = bass.Bass("TRN2", target_bir_lowering=False)
    nseg = len(segs)
    dr = {}

    def din(name, shape, dt=F32):
        dr[name] = nc.dram_tensor(name, list(shape), dt, kind="ExternalInput").ap()
        return dr[name]

    xs = [din(f"x{i}", [T, D]) for i, (T, H) in enumerate(segs)]
    mems = [din(f"mem{i}", [NMEM, D]) for i in range(nseg)]
    w_feat = din("w_feat", [NFG, 128, KC, GW])
    w_tok = din("w_tok", [NTG, 128, KC, GW])
    w_dt = din("w_dt", [128, KC, 32])
    w_out = din("w_out", [NOG, 128, KC, GW])
    w_mem = din("w_mem", [NMG, 128, KC, GW])
    g_norm = din("g_norm", [128, KC])
    g_ssd = din("g_ssd", [128, 8])
    g_mem = din("g_mem", [128, KC])
    convw_d = din("convw", [128, 12, 5])
    convb_d = din("convb", [128, 12])
    dtb_d = din("dtb", [1, 32])
    alog_d = din("alog", [1, 32])
    dskip_d = din("dskip", [1, 16])
    gq_d = din("gq", [1, 64])
    gk_d = din("gk", [1, 64])
    sink_d = din("sink", [1, 8])
    gmq_d = din("gmq", [1, 128])
    gmk_d = din("gmk", [1, 128])
    cmat_d = din("cmat", [128, 5, 128])
    alibi_d = din("alibi", [128, 6, 512])
    ys = [nc.dram_tensor(f"y{i}", [H, D], F32, kind="ExternalOutput").ap() for i, (T, H) in enumerate(segs)]
    wb_feat = nc.dram_tensor("wb_feat", [NFG, 128, KC, GW], BF16, kind="Internal").ap()
    wb_tok = nc.dram_tensor("wb_tok", [NTG, 128, KC, GW], BF16, kind="Internal").ap()
    wb_out = nc.dram_tensor("wb_out", [NOG, 128, KC, GW], BF16, kind="Internal").ap()
    wb_mem = nc.dram_tensor("wb_mem", [NMG, 128, KC, GW], BF16, kind="Internal").ap()
    max_own_chunks = max(H // 128 for T, H in segs)
    sbst = nc.dram_tensor("sbst", [max_own_chunks, 128, 1024], BF16, kind="Internal").ap()
    hn_st = nc.dram_tensor("hn_st", [max_own_chunks + 1, 128, KC, 128], BF16, kind="Internal").ap()
    cv_st = nc.dram_tensor("cv_st", [max_own_chunks // 4, 128, 10, 512], BF16, kind="Internal").ap()

    with ExitStack() as es:
        def sb(name, shape, dt):
            return es.enter_context(nc.sbuf_tensor(name, list(shape), dt))

        def ps(name, shape, dt):
            return es.enter_context(nc.psum_tensor(name, list(shape), dt))

        wbuf = sb("wbuf", [128, 2, KC, GW], BF16)
        hnT = sb("hnT", [128, 8, KC, 128], BF16)
        xin = sb("xin", [128, 2, D], F32)
        u = sb("u", [128, 12, 516], BF16)
        utail = sb("utail", [128, 12, 2], BF16)
        utail4 = sb("utail4", [128, 10, 6], BF16)
        cv = sb("cv", [128, 12, 512], BF16)
        xtok = sb("xtok", [128, 1280], BF16)
        zs = sb("zs", [128, 4, D], BF16)
        qn_all = sb("qn_all", [128, 4, 512], BF16)
        mqn_all = sb("mqn_all", [128, 4, 512], BF16)
        kT = sb("kT", [64, 8, 2, 128], BF16)
        vaug = sb("vaug", [128, 8, 2, 65], BF16)
        qT = sb("qT", [64, 8, 128], BF16)
        mqT = sb("mqT", [128, 4, 128], BF16)
        PT = sb("PT", [128, 3, 512], BF16)
        PmT = sb("PmT", [128, 2, 4, 128], BF16)
        mkT = sb("mkT", [128, 4, NMEM], BF16)
        mvaug = sb("mvaug", [128, 2, 4, 129], BF16)
        raw = sb("raw", [128, 256], F32)
        sqb = sb("sqb", [128, 256], F32)
        nrm_bf = sb("nrm_bf", [128, 256], BF16)
        ssn = sb("ssn", [128, 8], F32)
        rstdn = sb("rstdn", [128, 8], F32)
        ssx = sb("ssx", [128, 1], F32)
        rstdx = sb("rstdx", [128, 1], F32)
        _uf = u[:].rearrange("p a b -> p (a b)")
        Xd = _uf[:, 0:2048].rearrange("p (h n) -> p h n", n=128)
        Ed = _uf[:, 2048:4096].rearrange("p (h n) -> p h n", n=128)
        Mf = _uf[:, 4096:6144].rearrange("p (h n) -> p h n", n=128)
        UALL = [("u", cb_) for cb_ in range(12)]
        Mb = sb("Mb", [128, 16, 128], BF16)
        CBm = sb("CBm", [128, 2, 2, 128], BF16)
        xdt = sb("xdt", [128, 2, 1024], BF16)
        xw = sb("xw", [128, 1024], BF16)
        S = sb("S", [128, 1024], F32)
        Sbf = sb("Sbf", [128, 1024], BF16)
        sbin = sb("sbin", [128, 1024], BF16)
        yasm = sb("yasm", [128, 1024], F32)
        ytmp = sb("ytmp", [128, 512], F32)
        otmp = sb("otmp", [128, 256], F32)
        den4 = sb("den4", [128, 4, 1], F32)
        oall = sb("oall", [128, D], BF16)
        hn = oall
        oT = sb("oT", [128, 2, KC, 128], BF16)
        Dw = sb("Dw", [128, 12, 5, 128], BF16)
        xsk = sb("xsk", [128, 1024], BF16)
        cmat_f = sb("cmat_f", [128, 5, 128], F32)
        cmat_b = sb("cmat_b", [128, 5, 128], BF16)
        onesf = sb("onesf", [128, 128], F32)
        alibi_b = sb("alibi_b", [128, 6, 512], BF16)
        wdt = sb("wdt", [128, KC, 32], BF16)
        gn = sb("gn", [128, KC], F32)
        gs = sb("gs", [128, KC], F32)
        gm = sb("gm", [128, KC], F32)
        convw = sb("convw_s", [128, 12, 5], F32)
        convb = sb("convb_s", [128, 12], F32)
        dtb_bc = sb("dtb_bc", [128, 32], F32)
        A_bc = sb("A_bc", [128, 32], F32)
        dskip_bc = sb("dskip_bc", [128, 16], F32)
        gq_bc = sb("gq_bc", [128, 64], F32)
        gk_bc = sb("gk_bc", [128, 64], F32)
        esink = sb("esink", [128, 8], F32)
        gmq_bc = sb("gmq_bc", [128, 128], F32)
        gmk_bc = sb("gmk_bc", [128, 128], F32)
        dtmp = sb("dtmp", [128, 4, 32], F32)
        dtv = sb("dtv", [128, 4, 32], F32)
        a4 = sb("a4", [128, 4, 32], F32)
        cum = sb("cum", [128, 4, 32], F32)
        eac = sb("eac", [128, 4, 32], F32)
        wgt = sb("wgt", [128, 4, 32], F32)
        dA = sb("dA", [128, 4, 32], F32)
        ss2 = sb("ss2", [128, 2], F32)
        fence_t = sb("fence_t", [128, 2], F32)
        rs2 = sb("rs2", [128, 2], F32)
        psA = [ps(f"psA{i}", [128, 512], F32) for i in range(4)]
        psT = ps("psT", [128, 2048], BF16)
        psY = ps("psY", [128, 1024], F32)

        _zf = zs[:].rearrange("p a b -> p (a b)")
        cv2 = _zf[:, 0:5120].rearrange("p (c n) -> p c n", n=512)
        ZSK = [("zs", i_) for i_ in range(4)]
        _qf = qn_all[:].rearrange("p a b -> p (a b)")
        xtok2 = _qf[:, 0:1280]
        QNK = [("qn_all", i_) for i_ in range(4)]
        _pf = PT[:].rearrange("p a b -> p (a b)")
        xw2 = _pf[:, 0:1024]
        PTK = [("PT", o_) for o_ in range(3)]
        _mf = mqn_all[:].rearrange("p a b -> p (a b)").bitcast(F32)
        MQK = [("mqn_all", i_) for i_ in range(4)]

        _dwf = Dw[:].rearrange("p a b c -> p (a b c)")
        wbuf3 = _dwf[:, 0:KC * GW].rearrange("p (k n) -> p k n", n=GW)
        DWK = [("Dw", cb_) for cb_ in range(7)]

        def wsl_ap(wsl):
            return wbuf3 if wsl == 2 else wbuf[:, wsl]

        def wk(wsl):
            return [("wbuf", wsl)] + (DWK if wsl == 2 else [])

        class BS:
            pass

        bs0 = BS()
        bs0.cv, bs0.cvk = cv, (lambda cb: [("cv", cb)])
        bs0.dtmp, bs0.dtv, bs0.a4, bs0.cum, bs0.eac, bs0.wgt, bs0.dA = dtmp, dtv, a4, cum, eac, wgt, dA
        bs0.dk = lambda n_: [n_]
        bs1 = BS()
        bs1.cv, bs1.cvk = cv2, (lambda cb: [("cv2", cb)] + ZSK)
        (bs1.dtmp, bs1.dtv, bs1.a4, bs1.cum, bs1.eac, bs1.wgt, bs1.dA) = [
            _mf[:, j_ * 128:(j_ + 1) * 128].rearrange("p (c n) -> p c n", n=32) for j_ in range(7)]
        bs1.dk = lambda n_: [n_ + "_2"] + MQK
        bsa = [BS(), BS()]
        for b_src, b_dst, flat_ in ((bs0, bsa[0], cv[:].rearrange("p a b -> p (a b)")), (bs1, bsa[1], _zf)):
            b_dst.__dict__.update(b_src.__dict__)
            b_dst.cv = flat_[:, 0:5140].rearrange("p (c n) -> p c n", n=514)
        CVALL = [("cv", cb_) for cb_ in range(12)]
        xs0 = BS()
        xs0.xtok, xs0.k0, xs0.k1, xs0.xw, xs0.kw = xtok, ["xtok0"], ["xtok1"], xw, ["xw"]
        xs1 = BS()
        xs1.xtok, xs1.k0, xs1.k1, xs1.xw, xs1.kw = xtok2, ["xtok20"] + QNK, ["xtok21"] + QNK, xw2, ["xw2"] + PTK

        identf = cmat_f[:, 0, :]
        Tf_f = cmat_f[:, 1, :]
        Tb_f = cmat_f[:, 2, :]
        identb = cmat_b[:, 0, :]
        Tf_b = cmat_b[:, 1, :]
        Tb_b = cmat_b[:, 2, :]
        Lf_b = cmat_b[:, 3, :]
        Lb_b = cmat_b[:, 4, :]

        P = Prog(nc)
        st = {"acc": 0, "wslot": 0, "xi": 0, "ev": 0, "nws": 2, "ua": []}

        def fsz(ap):
            n = 1
            for d_ in ap.shape[1:]:
                n *= d_
            return n

        def MM(out, lhsT, rhs, start, stop, r, w):
            n = fsz(rhs)
            c = max(n, 64) / 2.05 + 13.0
            if rhs.dtype == F32:
                c *= 4
            P.op("pe", lambda e: e.matmul(out, lhsT=lhsT, rhs=rhs, start=start, stop=stop), r, w, cost=c)

        def TR(out, in_, r, w):
            P.op("pe", lambda e: e.transpose(out=out, in_=in_, identity=identb), list(r) + ["cmat_b"], w, cost=70.0)

        def ACT(out, in_, func, r, w, **kw):
            P.op("act", lambda e: e.activation(out=out, in_=in_, func=func, **kw), r, w, cost=224.0 + 0.84 * fsz(out))

        def ecost(eng, ap):
            n = fsz(ap)
            return (93.0 + 1.13 * n) if eng == "dve" else (200.0 + 1.73 * n)

        def TT(eng, out, in0, in1, op, r, w):
            P.op(eng, lambda e: e.tensor_tensor(out=out, in0=in0, in1=in1, op=op), r, w, cost=ecost(eng, out))

        def CP(eng, out, in_, r, w):
            if eng == "act":
                ACT(out, in_, AF.Copy, r, w)
            else:
                P.op(eng, lambda e: e.tensor_copy(out=out, in_=in_), r, w, cost=ecost(eng, out))

        def MSET(eng, ap, val, w):
            P.op(eng, lambda e: e.memset(ap, val), (), w, cost=0.5 * ecost(eng, ap))

        def DMA(q, out, in_, r, w):
            nb = fsz(out) * out.shape[0] * (2 if out.dtype == BF16 else 4)
            P.op(q, lambda e: e.dma_start(out=out, in_=in_), r, w, dma=True,
                 cost=(80.0 if q == "sp" else 900.0), lat=2000.0 + nb / 180.0)

        def next_acc():
            i = st["acc"]
            st["acc"] = (i + 1) % 4
            return psA[i], f"psA{i}"

        def evac_eng():
            st["ev"] ^= 1
            return "act" if st["ev"] else "dve"

        def bc(ap, shape, axis):
            return ap.unsqueeze(axis).broadcast_to(list(shape))

        def rsqrt_to(out_ap, in_ap, n, r, w):
            ACT(out_ap, in_ap, AF.Ln, r, w, scale=1.0 / n, bias=EPS)
            ACT(out_ap, out_ap, AF.Exp, w, w, scale=-0.5)

        DMA("sp", cmat_f[:], cmat_d, (), ["cmat_f"])
        CP("dve", cmat_b[:], cmat_f[:], ["cmat_f"], ["cmat_b"])
        MSET("pool", onesf[:], 1.0, ["onesf"])
        for h2 in range(2):
            for o3 in range(3):
                DMA("sp", xin[:, h2, o3 * 512:(o3 + 1) * 512], alibi_d[:, h2 * 3 + o3, :], (), [("xin", h2)])
            CP("dve", alibi_b[:, h2 * 3:(h2 + 1) * 3, :], xin[:, h2, 0:1536].rearrange("p (a n) -> p a n", n=512),
               [("xin", h2)], ["alibi_b"])
        DMA("sp", gn[:], g_norm, (), ["gn"])
        MSET("pool", gs[:], 1.0, ["gs"])
        DMA("sp", gs[:, 0:8], g_ssd, (), ["gs"])
        DMA("sp", gm[:], g_mem, (), ["gm"])
        DMA("sp", convw[:], convw_d, (), ["convw"])
        DMA("sp", convb[:], convb_d, (), ["convb"])
        for t_, d_, n_ in ((dtb_bc, dtb_d, "dtb_bc"), (A_bc, alog_d, "A_bc"), (dskip_bc, dskip_d, "dskip_bc"),
                           (gq_bc, gq_d, "gq_bc"), (gk_bc, gk_d, "gk_bc"), (esink, sink_d, "esink"),
                           (gmq_bc, gmq_d, "gmq_bc"), (gmk_bc, gmk_d, "gmk_bc")):
            DMA("sp", t_[:], d_.partition_broadcast(128), (), [n_])
        ACT(A_bc[:], A_bc[:], AF.Exp, ["A_bc"], ["A_bc"])
        P.op("dve", lambda e: e.tensor_scalar_mul(out=A_bc[:], in0=A_bc[:], scalar1=-1.0), ["A_bc"], ["A_bc"])
        ACT(esink[:], esink[:], AF.Exp, ["esink"], ["esink"])
        MSET("pool", vaug[:, :, :, 64:65], 1.0, [("vaug", s) for s in range(8)])
        MSET("pool", mvaug[:, :, :, 128:129], 1.0, ["mvaug"])

        engs3 = ["dve", "pool", "dve", "act"]

        def prep(src, dst, G, gcol, gname):
            for g in (range(G) if isinstance(G, int) else G):
                wsl = st["wslot"]
                st["wslot"] ^= 1
                for half in range(2):
                    DMA("sp", xin[:, half, :].rearrange("p (k n) -> p k n", n=GW), src[g, :, half * 8:(half + 1) * 8, :],
                        (), [("xin", half)])
                    eng = engs3[(2 * g + half) % 2 * 1]
                    eng = "dve" if half == 0 else "pool"
                    TT(eng, wsl_ap(wsl)[:, half * 8:(half + 1) * 8, :], xin[:, half, :].rearrange("p (k n) -> p k n", n=GW),
                       bc(gcol[:, half * 8:(half + 1) * 8], [128, 8, GW], 2), ALU.mult,
                       [("xin", half), gname], [*wk(wsl)])
                DMA("sp", dst[g], wsl_ap(wsl), [*wk(wsl)], [(dst.tensor.name, g)])

        prep(w_mem, wb_mem, NMG, gm, "gm")
        prep(w_feat, wb_feat, range(5), gn, "gn")

        _otf = oT[:].rearrange("p a k n -> p (a k n)").bitcast(F32)
        bg_in = _otf.rearrange("p (k n) -> p k n", n=GW)
        OTK = [("oT", a_, h_) for a_ in range(2) for h_ in range(2)]
        bg_out = yasm[:].bitcast(BF16).rearrange("p (k n) -> p k n", n=GW)

        def bg_prep_gen():
            jobs = [(w_feat, wb_feat, 5, gn, "gn")]
            jobs += [(w_tok, wb_tok, g_, gn, "gn") for g_ in range(NTG)]
            jobs += [(w_out, wb_out, g_, gs, "gs") for g_ in range(NOG)]
            n_ = 0
            for src, dst, g, gcol, gname in jobs:
                for half in range(2):
                    DMA("sp", bg_in, src[g, :, half * 8:(half + 1) * 8, :], (), OTK)
                    TT("dve" if n_ % 2 == 0 else "pool", bg_out, bg_in,
                       bc(gcol[:, half * 8:(half + 1) * 8], [128, 8, GW], 2), ALU.mult, OTK + [gname], [("yasm", 0), ("yasm", 1)])
                    DMA("sp", dst[g, :, half * 8:(half + 1) * 8, :], bg_out, [("yasm", 0), ("yasm", 1)], [(dst.tensor.name, g)])
                    n_ += 1
                    yield

        bg = bg_prep_gen()
        DMA("sp", xin[:, 0, 0:512].rearrange("p (k n) -> p k n", n=32), w_dt, (), [("xin", 0)])
        TT("dve", wdt[:], xin[:, 0, 0:512].rearrange("p (k n) -> p k n", n=32), bc(gn[:], [128, KC, 32], 2), ALU.mult,
           [("xin", 0), "gn"], ["wdt"])

        def load_w(dst, g):
            wsl = st["wslot"]
            st["wslot"] = (wsl + 1) % st["nws"]
            DMA("sp", wsl_ap(wsl), dst[g], [(dst.tensor.name, g)], [*wk(wsl)])
            return wsl

        def build_dw(cbs):
            for cb in cbs:
                for k in range(5):
                    P.op("dve", lambda e, cb=cb, k=k: e.tensor_scalar_mul(out=Dw[:, cb, k, :], in0=identf,
                                                                             scalar1=convw[:, cb, k:k + 1]),
                         ["cmat_f", "convw"], [("Dw", cb)], cost=180.0)

        ring = {}
        build_dw([10, 11])

        def norm_block(src_rows, slot):
            xi = st["xi"]
            st["xi"] ^= 1
            DMA("sp", xin[:, xi, :], src_rows, (), [("xin", xi)])
            MSET("pool", ssx[:], 0.0, ["ssx"])
            ACT(hn[:], xin[:, xi, :], AF.Square, [("xin", xi)], ["oall", "ssx"], accum_out=ssx[:])
            rsqrt_to(rstdx[:], ssx[:], D, ["ssx"], ["rstdx"])
            ACT(hn[:], xin[:, xi, :], AF.Identity, [("xin", xi), "rstdx"], ["oall"], scale=rstdx[:, 0:1])
            pv = psT[:].rearrange("p (k n) -> p k n", n=128)
            for k in range(KC):
                TR(pv[:, k, :], hn[:, k * 128:(k + 1) * 128], ["oall"], [f"psT{k // 8}"])
            CP("dve", hnT[:, slot, 0:8, :], pv[:, 0:8, :], ["psT0"], [("hnT", slot, 0)])
            CP("act", hnT[:, slot, 8:16, :], pv[:, 8:16, :], ["psT1"], [("hnT", slot, 1)])

        def ensure_hn(si, blk, store_upto=-1, load=False):
            slot = blk % 8
            if ring.get(slot) == (si, blk):
                return
            ring[slot] = (si, blk)
            if load:
                DMA("sp", hnT[:, slot], hn_st[blk], [("hn_st", blk)], [("hnT", slot, 0), ("hnT", slot, 1)])
                return
            norm_block(xs[si][blk * 128:(blk + 1) * 128, :], slot)
            if blk <= store_upto:
                DMA("sp", hn_st[blk], hnT[:, slot], [("hnT", slot, 0), ("hnT", slot, 1)], [("hn_st", blk)])

        def headnorm(acc, accn, nh, hd, gbc, gname, out_ap, out_keys):
            w_ = nh * hd
            CP("act", raw[:, 0:w_], acc[:, 0:w_], [accn], ["raw"])
            ACT(sqb[:, 0:w_], acc[:, 0:w_], AF.Square, [accn], ["sqb"])
            P.op("dve", lambda e: e.tensor_reduce(out=ssn[:, 0:nh], in_=sqb[:, 0:w_].rearrange("p (h d) -> p h d", d=hd),
                                                   axis=AX.X, op=ALU.add), ["sqb"], ["ssn"], cost=70.0 + 0.85 * w_)
            rsqrt_to(rstdn[:, 0:nh], ssn[:, 0:nh], hd, ["ssn"], ["rstdn"])
            r3 = raw[:, 0:w_].rearrange("p (h d) -> p h d", d=hd)
            TT("dve", r3, r3, bc(rstdn[:, 0:nh], [128, nh, hd], 2), ALU.mult, ["raw", "rstdn"], ["raw"])
            TT("pool", out_ap, r3, bc(gbc[:, 0:hd], [128, nh, hd], 1), ALU.mult, ["raw", gname], out_keys)

        def mem_kv(si):
            for mb in range(2):
                ring[6 + mb] = None
                norm_block(mems[si][mb * 128:(mb + 1) * 128, :], 6 + mb)
            for grp in range(NMG):
                wsl = load_w(wb_mem, grp)
                for mb in range(2):
                    acc, accn = next_acc()
                    for k in range(KC):
                        MM(acc[:, 0:GW], hnT[:, 6 + mb, k, :], wsl_ap(wsl)[:, k, :], k == 0, k == KC - 1,
                           [("hnT", 6 + mb, k // 8), *wk(wsl)], [accn])
                    if grp < 2:
                        headnorm(acc, accn, 2, 128, gmk_bc, "gmk_bc",
                                 nrm_bf[:, 0:256].rearrange("p (h d) -> p h d", d=128), ["nrm_bf"])
                        for hh in range(2):
                            TR(psT[:, 0:128], nrm_bf[:, hh * 128:(hh + 1) * 128], ["nrm_bf"], ["psT0"])
                            CP("dve", mkT[:, grp * 2 + hh, mb * 128:(mb + 1) * 128], psT[:, 0:128], ["psT0"], ["mkT"])
                    else:
                        g2 = grp - 2
                        CP("act", mvaug[:, mb, 2 * g2:2 * g2 + 2, 0:128], acc[:, 0:256].rearrange("p (h d) -> p h d", d=128),
                           [accn], ["mvaug"])

        def feat_inproj(si, b0, groups, asc, NB, first):
            ua = st["ua"]
            s0 = b0 % 8
            ahead = b0 + 4 if asc else b0 - 1
            has_ahead = 0 <= ahead < NB
            for grp in groups:
                wsl = load_w(wb_feat, grp)
                for j in range(2):
                    cb = 2 * grp + j
                    bh = u[:, cb, 0:2] if asc else u[:, cb, 514:516]
                    if first:
                        MSET("pool", bh, 0.0, [("u", cb)] + ua)
                    else:
                        CP("pool", bh, utail[:, cb, :], [("utail", cb)], [("u", cb)] + ua)
                    acc, accn = next_acc()
                    lw = wsl_ap(wsl)[:, :, j * 128:(j + 1) * 128]
                    for k in range(KC):
                        MM(acc[:].rearrange("p (s n) -> p s n", n=128), lw[:, k, :], hnT[:, s0:s0 + 4, k, :],
                           k == 0, k == KC - 1, [("hnT", s0 + i, k // 8) for i in range(4)] + [*wk(wsl)], [accn])
                    CP("act", u[:, cb, 2:514], acc[:], [accn], [("u", cb)] + ua)
                    ah = u[:, cb, 514:516] if asc else u[:, cb, 0:2]
                    if has_ahead:
                        sa = ahead % 8
                        acc2, acc2n = next_acc()
                        tsl = slice(0, 2) if asc else slice(126, 128)
                        for k in range(KC):
                            MM(acc2[:, 0:2], lw[:, k, :], hnT[:, sa, k, tsl], k == 0, k == KC - 1,
                               [("hnT", sa, k // 8), *wk(wsl)], [acc2n])
                        CP("dve", ah, acc2[:, 0:2], [acc2n], [("u", cb)] + ua)
                    else:
                        MSET("pool", ah, 0.0, [("u", cb)] + ua)
                    sv = u[:, cb, 512:514] if asc else u[:, cb, 2:4]
                    CP("pool", utail[:, cb, :], sv, [("u", cb)], [("utail", cb)] + ua)

        def conv(cbs, bs=None, a_layout=False):
            bs = bs or bs0
            for cb in cbs:
                acc, accn = next_acc()
                for k in range(5):
                    MM(acc[:], Dw[:, cb, k, :], u[:, cb, k:k + 512], k == 0, k == 4, [("Dw", cb), ("u", cb)], [accn] + st["ua"])
                dst = bs.cv[:, cb, 2:514] if a_layout else bs.cv[:, cb, :]
                ACT(dst, acc[:], AF.Silu, [accn, "convb"], bs.cvk(cb), bias=convb[:, cb:cb + 1])

        def feat_inproj_a(si, b0, first):
            s0 = b0 % 8
            for grp in range(5):
                wsl = load_w(wb_feat, grp)
                for j in range(2):
                    cb = 2 * grp + j
                    if first:
                        MSET("pool", u[:, cb, 512:516], 0.0, [("u", cb)])
                    else:
                        CP("pool", u[:, cb, 512:516], utail4[:, cb, 2:6], [("utail4", cb)], [("u", cb)])
                    acc, accn = next_acc()
                    lw = wsl_ap(wsl)[:, :, j * 128:(j + 1) * 128]
                    for k in range(KC):
                        MM(acc[:].rearrange("p (s n) -> p s n", n=128), lw[:, k, :], hnT[:, s0:s0 + 4, k, :],
                           k == 0, k == KC - 1, [("hnT", s0 + i, k // 8) for i in range(4)] + [*wk(wsl)], [accn])
                    CP("act", u[:, cb, 0:512], acc[:], [accn], [("u", cb)])
                    CP("pool", utail4[:, cb, 2:6], u[:, cb, 0:4], [("u", cb)], [("utail4", cb)])

        def dt_prep(si, b0, bs=None):
            bs = bs or bs0
            dk = bs.dk
            dtmp, dtv, a4, cum, eac, wgt, dA = bs.dtmp, bs.dtv, bs.a4, bs.cum, bs.eac, bs.wgt, bs.dA
            acc, accn = next_acc()
            pd = acc[:, 0:128].rearrange("p (c n) -> p c n", n=32)
            for i in range(4):
                sl = (b0 + i) % 8
                for k in range(KC):
                    MM(pd[:, i, :], hnT[:, sl, k, :], wdt[:, k, :], k == 0, k == KC - 1, [("hnT", sl, k // 8), "wdt"], [accn])
            TT("dve", dtmp[:], pd, bc(dtb_bc[:], [128, 4, 32], 1), ALU.add, [accn, "dtb_bc"], dk("dtmp"))
            ACT(dtmp[:], dtmp[:], AF.Exp, dk("dtmp"), dk("dtmp"))
            ACT(dtv[:], dtmp[:], AF.Ln, dk("dtmp"), dk("dtv"), bias=1.0)
            TT("dve", a4[:], dtv[:], bc(A_bc[:], [128, 4, 32], 1), ALU.mult, dk("dtv") + ["A_bc"], dk("a4"))
            acc, accn = next_acc()
            pc = acc[:, 0:384].rearrange("p (c j n) -> p c j n", j=3, n=32)
            for i in range(4):
                for j, lh in enumerate((Tf_f, Tb_f, onesf[:])):
                    MM(pc[:, i, j, :], lh, a4[:, i, :], True, True, ["cmat_f", "onesf"] + dk("a4"), [accn])
            CP("dve", cum[:, :, 0:16], pc[:, :, 0, 0:16], [accn], dk("cum"))
            CP("dve", cum[:, :, 16:32], pc[:, :, 1, 16:32], [accn], dk("cum"))
            ACT(eac[:], cum[:], AF.Exp, dk("cum"), dk("eac"))
            TT("dve", dtmp[:], pc[:, :, 2, :], cum[:], ALU.subtract, [accn] + dk("cum"), dk("dtmp"))
            ACT(dtmp[:], dtmp[:], AF.Exp, dk("dtmp"), dk("dtmp"))
            TT("dve", wgt[:], dtv[:], dtmp[:], ALU.mult, dk("dtv") + dk("dtmp"), dk("wgt"))
            ACT(dA[:], pc[:, :, 2, :], AF.Exp, [accn], dk("dA"))

        def chunk_transposes(c, ncb, bs=None, xs_=None):
            bs = bs or bs0
            xs_ = xs_ or xs0
            for cb in range(ncb):
                TR(psT[:, cb * 128:(cb + 1) * 128], bs.cv[:, cb, c * 128:(c + 1) * 128], bs.cvk(cb), [f"psT{cb // 8}"])
            CP("dve", xs_.xtok[:, 0:1024], psT[:, 0:1024], ["psT0"], xs_.k0)
            CP("act", xs_.xtok[:, 1024:1280], psT[:, 1024:1280], ["psT1"], xs_.k1)

        def state_update(c, d0, bs=None, xs_=None):
            bs = bs or bs0
            xs_ = xs_ or xs0
            xtok_, xw_ = xs_.xtok, xs_.xw
            x3 = xtok_[:, 0:1024].rearrange("p (h d) -> p h d", d=64)
            TT("pool", xw_[:].rearrange("p (h d) -> p h d", d=64), x3, bc(bs.wgt[:, c, d0:d0 + 16], [128, 16, 64], 2), ALU.mult,
               xs_.k0 + bs.dk("wgt"), xs_.kw)
            for g in range(2):
                acc, accn = next_acc()
                MM(acc[:], xtok_[:, 1024 + g * 128:1024 + (g + 1) * 128], xw_[:, g * 512:(g + 1) * 512], True, True,
                   xs_.k1 + xs_.kw, [accn])
                Sg = S[:, g * 512:(g + 1) * 512]
                S3 = Sg.rearrange("p (h d) -> p h d", d=64)
                TT("dve", S3, S3, bc(bs.dA[:, c, d0 + g * 8:d0 + g * 8 + 8], [128, 8, 64], 2), ALU.mult,
                   [("S", g)] + bs.dk("dA"), [("S", g)])
                TT("dve", Sg, Sg, acc[:], ALU.add, [("S", g), accn], [("S", g)])
                CP("act", Sbf[:, g * 512:(g + 1) * 512], Sg, [("S", g)], [("Sbf", g)])

        def sweep_a(si):
            T, H = segs[si]
            NB = T // 128
            own_chunks = H // 128
            MSET("pool", S[:], 0.0, [("S", 0), ("S", 1)])
            MSET("pool", Sbf[:], 0.0, [("Sbf", 0), ("Sbf", 1)])
            ntiles = NB // 4

            def cvkeys(bs):
                return [k_ for cb_ in range(10) for k_ in bs.cvk(cb_)]

            def chunk(ti, c):
                bs = bsa[ti % 2]
                ch = ti * 4 + c
                xs_ = (xs0, xs1)[c % 2]
                if ch < own_chunks:
                    DMA("sp", sbst[ch], Sbf[:], [("Sbf", 0), ("Sbf", 1)], [("sbst", ch)])
                if ch == 0:
                    return
                chunk_transposes(c, 10, bs, xs_)
                state_update(c, 16, bs, xs_)

            def finish_tile(tj):
                bsj = bsa[tj % 2]
                if tj * 4 < own_chunks:
                    DMA("sp", cv_st[tj], bsj.cv[:, :, 0:512], cvkeys(bsj), [("cv_st", tj)])
                chunk(tj, 0)

            for ti in range(ntiles - 1, -1, -1):
                b0 = ti * 4
                bs = bsa[ti % 2]
                for blk in range(b0 + 3, b0 - 1, -1):
                    ensure_hn(si, blk, store_upto=own_chunks)
                for _ in range(3):
                    next(bg, None)
                feat_inproj_a(si, b0, ti == ntiles - 1)
                conv(range(10), bs, a_layout=True)
                if ti + 1 < ntiles:
                    bsn = bsa[(ti + 1) % 2]
                    CP("dve", bsn.cv[:, :, 0:2], bs.cv[:, :, 512:514], cvkeys(bs), cvkeys(bsn))
                    finish_tile(ti + 1)
                dt_prep(si, b0, bs)
                for c in range(3, 0, -1):
                    chunk(ti, c)
            bs = bsa[0]
            MSET("pool", utail4[:, :, 0:2], 0.0, [("utail4", cb_) for cb_ in range(10)])
            for cb in range(10):
                acc, accn = next_acc()
                for k in range(5):
                    MM(acc[:, 0:2], Dw[:, cb, k, :], utail4[:, cb, k:k + 2], k == 0, k == 4, [("Dw", cb), ("utail4", cb)], [accn])
                ACT(bs.cv[:, cb, 0:2], acc[:, 0:2], AF.Silu, [accn, "convb"], bs.cvk(cb), bias=convb[:, cb:cb + 1])
            finish_tile(0)

        def tok_inproj(si, b0, NB):
            if DEBUG_STOP == "b_tok1":
                return
            for grp in range(2):
                wsl = load_w(wb_tok, TG_Q0 + grp)
                for i in range(4):
                    sl = (b0 + i) % 8
                    acc, accn = next_acc()
                    for k in range(KC):
                        MM(acc[:, 0:GW], hnT[:, sl, k, :], wsl_ap(wsl)[:, k, :], k == 0, k == KC - 1,
                           [("hnT", sl, k // 8), *wk(wsl)], [accn])
                    headnorm(acc, accn, 4, 64, gq_bc, "gq_bc",
                             qn_all[:, i, grp * 256:(grp + 1) * 256].rearrange("p (h d) -> p h d", d=64), [("qn_all", i)])
            if DEBUG_STOP == "b_tok2":
                return
            for grp in range(2):
                wsl = load_w(wb_tok, TG_MQ0 + grp)
                for i in range(4):
                    sl = (b0 + i) % 8
                    acc, accn = next_acc()
                    for k in range(KC):
                        MM(acc[:, 0:GW], hnT[:, sl, k, :], wsl_ap(wsl)[:, k, :], k == 0, k == KC - 1,
                           [("hnT", sl, k // 8), *wk(wsl)], [accn])
                    headnorm(acc, accn, 2, 128, gmq_bc, "gmq_bc",
                             mqn_all[:, i, grp * 256:(grp + 1) * 256].rearrange("p (h d) -> p h d", d=128), [("mqn_all", i)])
            if DEBUG_STOP == "b_tok3":
                return
            wsl = load_w(wb_tok, TG_KV)
            blks = [b for b in range(b0, min(b0 + 5, NB)) if kvring.get(b % 8) != (si, b)]
            for blk in blks:
                sl = blk % 8
                kvring[sl] = (si, blk)
                acc, accn = next_acc()
                for k in range(KC):
                    MM(acc[:, 0:GW], hnT[:, sl, k, :], wsl_ap(wsl)[:, k, :], k == 0, k == KC - 1,
                       [("hnT", sl, k // 8), *wk(wsl)], [accn])
                if DEBUG_STOP == "b_kv0":
                    return
                CP("dve", vaug[:, sl, :, 0:64], acc[:, 128:256].rearrange("p (g d) -> p g d", d=64), [accn], [("vaug", sl)])
                if DEBUG_STOP == "b_kv1":
                    return
                headnorm(acc, accn, 2, 64, gk_bc, "gk_bc", nrm_bf[:, 0:128].rearrange("p (h d) -> p h d", d=64), ["nrm_bf"])
                if DEBUG_STOP == "b_kv2":
                    return
                for g in range(2):
                    TR(psT[0:64, g * 128:(g + 1) * 128], nrm_bf[:, g * 64:(g + 1) * 64], ["nrm_bf"], ["psT0"])
                if DEBUG_STOP == "b_kv3":
                    return
                CP("dve", kT[:, sl, :, :], psT[0:64, 0:256].rearrange("p (g n) -> p g n", n=128), ["psT0"], [("kT", sl)])
                if DEBUG_STOP == "b_kv4":
                    return
                if DEBUG_STOP is not None and DEBUG_STOP.startswith("b_kvn") and blk - b0 + 1 == int(DEBUG_STOP[5:]):
                    return

        def gates_inproj(si, b0):
            for grp in range(8):
                wsl = load_w(wb_tok, grp)
                for i in range(4):
                    sl = (b0 + i) % 8
                    acc, accn = next_acc()
                    for k in range(KC):
                        MM(acc[:, 0:GW], hnT[:, sl, k, :], wsl_ap(wsl)[:, k, :], k == 0, k == KC - 1,
                           [("hnT", sl, k // 8), *wk(wsl)], [accn])
                    ACT(zs[:, i, grp * GW:(grp + 1) * GW], acc[:, 0:GW], AF.Silu, [accn], [("zs", i)])

        def attention(si, b0, i, NB):
            qb = b0 + i
            for h in range(8):
                TR(psT[0:64, h * 128:(h + 1) * 128], qn_all[:, i, h * 64:(h + 1) * 64], [("qn_all", i)], ["psT0"])
            CP("dve", qT[:], psT[0:64, 0:1024].rearrange("p (h n) -> p h n", n=128), ["psT0"], ["qT"])
            kbs = [kb for kb in (qb - 1, qb, qb + 1) if 0 <= kb < NB]
            for g in range(2):
                for kb in kbs:
                    o = kb - qb + 1
                    sl = kb % 8
                    acc, accn = next_acc()
                    MM(acc[:].rearrange("p (h n) -> p h n", n=128), kT[:, sl, g, :], qT[:, 4 * g:4 * g + 4, :], True, False,
                       [("kT", sl), "qT"], [accn])
                    MM(acc[:], identb, alibi_b[:, o * 2 + g, :], False, True, ["cmat_b", "alibi_b"], [accn])
                    ACT(PT[:, o, :], acc[:], AF.Exp, [accn], [("PT", o)], scale=0.125)
                acc, accn = next_acc()
                po = acc[:, 0:260].rearrange("p (h n) -> p h n", n=65)
                for r_ in range(4):
                    for n_, kb in enumerate(kbs):
                        o = kb - qb + 1
                        sl = kb % 8
                        MM(po[:, r_, :], PT[:, o, r_ * 128:(r_ + 1) * 128], vaug[:, sl, g, :], n_ == 0, n_ == len(kbs) - 1,
                           [("PT", o), ("vaug", sl)], [accn])
                TT("dve", den4[:], po[:, :, 64:65], esink[:, 4 * g:4 * g + 4].unsqueeze(2), ALU.add, [accn, "esink"], ["den4"])
                P.op("dve", lambda e: e.reciprocal(out=den4[:], in_=den4[:]), ["den4"], ["den4"])
                o3 = otmp[:].rearrange("p (h d) -> p h d", d=64)
                TT("dve", o3, po[:, :, 0:64], den4[:].broadcast_to([128, 4, 64]), ALU.mult, [accn, "den4"], ["otmp"])
                c0 = 1024 + g * 256
                TT("pool", oall[:, c0:c0 + 256], otmp[:], zs[:, i, c0:c0 + 256], ALU.mult, ["otmp", ("zs", i)], ["oall"])

        def mem_attention(i):
            for h in range(4):
                TR(psT[:, h * 128:(h + 1) * 128], mqn_all[:, i, h * 128:(h + 1) * 128], [("mqn_all", i)], ["psT0"])
            CP("act", mqT[:], psT[:, 0:512].rearrange("p (h n) -> p h n", n=128), ["psT0"], ["mqT"])
            for mc in range(2):
                acc, accn = next_acc()
                a3 = acc[:].rearrange("p (h n) -> p h n", n=128)
                for h in range(4):
                    MM(a3[:, h, :], mkT[:, h, mc * 128:(mc + 1) * 128], mqT[:, h, :], True, True, ["mkT", "mqT"], [accn])
                ACT(PmT[:, mc, :, :], a3, AF.Exp, [accn], ["PmT"], scale=128.0 ** -0.5)
            for hp in range(2):
                acc, accn = next_acc()
                po = acc[:, 0:258].rearrange("p (h n) -> p h n", n=129)
                for hh in range(2):
                    h = hp * 2 + hh
                    for mc in range(2):
                        MM(po[:, hh, :], PmT[:, mc, h, :], mvaug[:, mc, h, :], mc == 0, mc == 1, ["PmT", "mvaug"], [accn])
                P.op("dve", lambda e, po=po: e.reciprocal(out=den4[:, 0:2, :], in_=po[:, :, 128:129]), [accn], ["den4"])
                o3 = otmp[:].rearrange("p (h d) -> p h d", d=128)
                TT("dve", o3, po[:, :, 0:128], den4[:, 0:2, :].broadcast_to([128, 2, 128]), ALU.mult, [accn, "den4"], ["otmp"])
                c0 = 1536 + hp * 256
                TT("pool", oall[:, c0:c0 + 256], otmp[:], zs[:, i, c0:c0 + 256], ALU.mult, ["otmp", ("zs", i)], ["oall"])

        def ssd_chunk(si, b0, c, before_gate=None):
            ch = b0 + c
            DMA("sp", sbin[:], sbst[ch], [("sbst", ch)], ["sbin"])
            chunk_transposes(c, 10)
            x3 = xtok[:, 0:1024].rearrange("p (h d) -> p h d", d=64)
            for d in range(2):
                TT("pool", xdt[:, d, :].rearrange("p (h d) -> p h d", d=64), x3,
                   bc(dtv[:, c, d * 16:(d + 1) * 16], [128, 16, 64], 2), ALU.mult, ["xtok0", "dtv"], [("xdt", d)])
            TT("pool", xsk[:].rearrange("p (h d) -> p h d", d=64), x3, bc(dskip_bc[:], [128, 16, 64], 2), ALU.mult,
               ["xtok0", "dskip_bc"], ["xsk"])
            acc, accn = next_acc()
            cbp = acc[:, 0:256].rearrange("p (g n) -> p g n", n=128)
            for g in range(2):
                MM(cbp[:, g, :], cv[:, 8 + g, c * 128:(c + 1) * 128], cv[:, 10 + g, c * 128:(c + 1) * 128], True, True,
                   [("cv", 8 + g), ("cv", 10 + g)], [accn])
            TT("dve", CBm[:, 0, :, :], cbp, bc(Tf_b, [128, 2, 128], 1), ALU.mult, [accn, "cmat_b"], ["CBm"])
            TT("dve", CBm[:, 1, :, :], cbp, bc(Tb_b, [128, 2, 128], 1), ALU.mult, [accn, "cmat_b"], ["CBm"])
            for d, (Md, Mn, Ud, Ld) in enumerate(((Mf, "Mf", Tf_b, Lf_b), (Mb, "Mb", Tb_b, Lb_b))):
                ua = []
                for hq in range(4):
                    hs4 = slice(4 * hq, 4 * hq + 4)
                    TT("pool" if (hq + d) % 2 == 0 else "dve", Xd[:, hs4, :],
                       bc(a4[:, c, d * 16 + 4 * hq:d * 16 + 4 * hq + 4], [128, 4, 128], 2), bc(Ud, [128, 4, 128], 1), ALU.mult,
                       ["a4", "cmat_b", "UA"], [("Xd", hq)])
                    acc, accn = next_acc()
                    MM(acc[:].rearrange("p (h n) -> p h n", n=128), Ld, Xd[:, hs4, :], True, True,
                       ["cmat_b", ("Xd", hq)], [accn])
                    ACT(Ed[:, hs4, :], acc[:].rearrange("p (h n) -> p h n", n=128), AF.Exp, [accn, "UA"], [("Ed", hq)])
                    TT("dve", Md[:, hs4, :], Ed[:, hs4, :], bc(CBm[:, d, hq // 2, :], [128, 4, 128], 1), ALU.mult,
                       [("Ed", hq), "CBm", "UA"], [(Mn, hq)])
            for h in range(16):
                hs = slice(h * 64, (h + 1) * 64)
                yk = [f"psY{h // 8}"]
                MM(psY[:, hs], Mf[:, h, :], xdt[:, 0, hs], True, False, [("Mf", h // 4), ("xdt", 0)], yk)
                MM(psY[:, hs], Mb[:, h, :], xdt[:, 1, hs], False, False, [("Mb", h // 4), ("xdt", 1)], yk)
                MM(psY[:, hs], identb, xsk[:, hs], False, True, ["cmat_b", "xsk"], yk)
            MSET("pool", ss2[:], 0.0, [("ss2", 0), ("ss2", 1)])
            for g in range(2):
                gs_ = slice(g * 512, (g + 1) * 512)
                yk_ = [("yasm", g)]
                accf, accfn = next_acc()
                MM(accf[:], cv[:, 10 + g, c * 128:(c + 1) * 128], Sbf[:, gs_], True, True, [("cv", 10 + g), ("Sbf", g)], [accfn])
                accb, accbn = next_acc()
                MM(accb[:], cv[:, 10 + g, c * 128:(c + 1) * 128], sbin[:, gs_], True, True, [("cv", 10 + g), "sbin"], [accbn])
                y3 = yasm[:, gs_].rearrange("p (h d) -> p h d", d=64)
                t3 = ytmp[:].rearrange("p (h d) -> p h d", d=64)
                TT("dve", y3, accf[:].rearrange("p (h d) -> p h d", d=64), bc(eac[:, c, g * 8:g * 8 + 8], [128, 8, 64], 2), ALU.mult,
                   [accfn, "eac"], yk_)
                TT("dve", t3, accb[:].rearrange("p (h d) -> p h d", d=64), bc(eac[:, c, 16 + g * 8:16 + g * 8 + 8], [128, 8, 64], 2),
                   ALU.mult, [accbn, "eac"], ["ytmp"])
                TT("pool", yasm[:, gs_], yasm[:, gs_], ytmp[:], ALU.add, yk_ + ["ytmp"], yk_)
                TT("dve", yasm[:, gs_], yasm[:, gs_], psY[:, gs_], ALU.add, yk_ + [f"psY{g}"], yk_)
            if before_gate is not None:
                before_gate()
            for g in range(2):
                gs_ = slice(g * 512, (g + 1) * 512)
                yk_ = [("yasm", g)]
                TT("pool", yasm[:, gs_], yasm[:, gs_], zs[:, c, gs_], ALU.mult, yk_ + [("zs", c)], yk_)
                ACT(ytmp[:], yasm[:, gs_], AF.Square, yk_, ["ytmp", ("ss2", g)], accum_out=ss2[:, g:g + 1])
                rsqrt_to(rs2[:, g:g + 1], ss2[:, g:g + 1], 512, [("ss2", g)], [("rs2", g)])
                ACT(oall[:, gs_], yasm[:, gs_], AF.Identity, yk_ + [("rs2", g)], ["oall"], scale=rs2[:, g:g + 1])
            state_update(c, 0)

        def tail(si, b0, half):
            pv = psT[:].rearrange("p (k n) -> p k n", n=128)
            for ii in range(2):
                blk = b0 + 2 * half + ii
                DMA("sp", xin[:, ii, :], xs[si][blk * 128:(blk + 1) * 128, :], (), [("xin", ii)])
            for grp in range(NOG):
                wsl = load_w(wb_out, grp)
                for ii in range(2):
                    acc, accn = next_acc()
                    for k in range(KC):
                        MM(acc[:, 0:GW], oT[:, ii, k, :], wsl_ap(wsl)[:, k, :], k == 0, k == KC - 1,
                           [("oT", ii, k // 8), *wk(wsl)], [accn])
                    xsl = xin[:, ii, grp * GW:(grp + 1) * GW]
                    TT("dve", xsl, acc[:, 0:GW], xsl, ALU.add, [accn, ("xin", ii)], [("xin", ii)])
            for ii in range(2):
                blk = b0 + 2 * half + ii
                DMA("sp", ys[si][blk * 128:(blk + 1) * 128, :], xin[:, ii, :], [("xin", ii)], [("y", si, blk)])

        def oall_to_oT(ii):
            pv = psT[:].rearrange("p (k n) -> p k n", n=128)
            for k in range(KC):
                TR(pv[:, k, :], oall[:, k * 128:(k + 1) * 128], ["oall"], [f"psT{k // 8}"])
            CP("dve", oT[:, ii, 0:8, :], pv[:, 0:8, :], ["psT0"], [("oT", ii, 0)])
            CP("act", oT[:, ii, 8:16, :], pv[:, 8:16, :], ["psT1"], [("oT", ii, 1)])

        kvring = {}

        def sweep_b(si):
            T, H = segs[si]
            NB = T // 128
            MSET("pool", S[:], 0.0, [("S", 0), ("S", 1)])
            MSET("pool", Sbf[:], 0.0, [("Sbf", 0), ("Sbf", 1)])
            ntiles = (H // 128) // 4
            for ti in range(ntiles):
                b0 = ti * 4
                for blk in range(b0, min(b0 + 4, NB - 1) + 1):
                    ensure_hn(si, blk, load=True)
                DMA("sp", cv[:, 0:10, :], cv_st[ti], [("cv_st", ti)], [("cv", cb_) for cb_ in range(10)])
                feat_inproj(si, b0, [5], True, NB, ti == 0)
                conv([10, 11])
                if DEBUG_STOP == "b_feat":
                    return
                tok_inproj(si, b0, NB)
                dt_prep(si, b0)
                if DEBUG_STOP in ("b_tok", "b_tok1", "b_tok2", "b_tok3", "b_kv0", "b_kv1", "b_kv2", "b_kv3", "b_kv4") or (DEBUG_STOP or "").startswith("b_kvn"):
                    return
                for half in range(2):
                    for ii in range(2):
                        i = 2 * half + ii
                        if i == 0:
                            ssd_chunk(si, b0, i, lambda: gates_inproj(si, b0))
                            attention(si, b0, i, NB)
                            mem_attention(i)
                        else:
                            attention(si, b0, i, NB)
                            mem_attention(i)
                            ssd_chunk(si, b0, i)
                        oall_to_oT(ii)
                    tail(si, b0, half)
                    if DEBUG_STOP == "b_tail":
                        return

        for si in range(nseg):
            if DEBUG_STOP in ("setup", "prep"):
                break
            ring.clear()
            kvring.clear()
            mem_kv(si)
            ring.clear()
            if DEBUG_STOP == "mem":
                break
            st["nws"] = 2
            st["wslot"] = st["wslot"] % 2
            build_dw(range(10))
            st["ua"] = []
            MSET("pool", fence_t[:], 0.0, ["fence_t", "UA"] + UALL + CVALL)
            sweep_a(si)
            if DEBUG_STOP == "sweepa":
                break
            for _ in bg:
                pass
            st["nws"] = 3
            st["ua"] = ["UA"]
            MSET("pool", fence_t[:], 0.0, ["fence_t", "UA"] + UALL + CVALL)
            sweep_b(si)
            st["nws"] = 2
            st["wslot"] = st["wslot"] % 2

        P.emit()
    return nc


def _grp(W):
    C = W.shape[1]
    return np.ascontiguousarray(W.reshape(KC, 128, C // GW, GW).transpose(2, 1, 0, 3))


def _consts():
    i = np.arange(128)
    ident = np.eye(128, dtype=np.float32)
    Tf = (i[:, None] <= i[None, :]).astype(np.float32)
    Tb = (i[:, None] >= i[None, :]).astype(np.float32)
    Lf = (i[:, None] > i[None, :]).astype(np.float32)
    Lb = (i[:, None] < i[None, :]).astype(np.float32)
    cmat = np.ascontiguousarray(np.stack([ident, Tf, Tb, Lf, Lb], 1))
    al = np.zeros((128, 6, 512), np.float32)
    s_ = i[:, None]
    t_ = i[None, :]
    for o in range(3):
        dist = np.abs(t_ - s_ - 128 * (o - 1)).astype(np.float32)
        for g in range(2):
            for r in range(4):
                h = 4 * g + r
                slope = 2.0 ** (-(h + 1))
                v = np.where(dist <= 128, -8.0 * slope * dist, -240000.0)
                al[:, o * 2 + g, r * 128:(r + 1) * 128] = v
    return cmat, al


def _core_params(p, rev):
    w_in = p["w_in"]
    xbc, zssd, dtc, q, k, v, zatt, mq, zmem = np.split(w_in, np.cumsum([1536, 1024, 32, 512, 128, 128, 512, 512])[:], axis=1)
    dt_f, dt_b = dtc[:, 0:16], dtc[:, 16:32]
    dtb = p["dt_bias"].reshape(2, 16)
    alog = p["a_log"].reshape(2, 16)
    convw = p["conv_w"]
    if rev:
        dt_f, dt_b = dt_b, dt_f
        dtb = dtb[::-1]
        alog = alog[::-1]
        convw = convw[::-1]
    w_tok = np.concatenate([zssd, zatt, zmem, q, mq, k, v], axis=1)
    cmat, al = _consts()
    f = np.float32
    out = {
        "w_feat": _grp(xbc),
        "w_tok": _grp(w_tok),
        "w_dt": np.ascontiguousarray(np.concatenate([dt_f, dt_b], 1).reshape(KC, 128, 32).transpose(1, 0, 2)),
        "w_out": _grp(p["w_out"]),
        "w_mem": _grp(p["w_mem_kv"]),
        "g_norm": np.ascontiguousarray(p["norm_g"].reshape(KC, 128).T),
        "g_ssd": np.ascontiguousarray(p["ssd_norm_g"].reshape(8, 128).T),
        "g_mem": np.ascontiguousarray(p["mem_norm_g"].reshape(KC, 128).T),
        "convw": np.ascontiguousarray(convw.reshape(5, 12, 128).transpose(2, 1, 0)),
        "convb": np.ascontiguousarray(p["conv_b"].reshape(12, 128).T),
        "dtb": np.ascontiguousarray(dtb.reshape(1, 32)),
        "alog": np.ascontiguousarray(alog.reshape(1, 32)),
        "dskip": p["d_skip"].reshape(1, 16),
        "gq": p["q_norm_g"].reshape(1, 64),
        "gk": p["k_norm_g"].reshape(1, 64),
        "sink": p["sink"].reshape(1, 8),
        "gmq": p["mq_norm_g"].reshape(1, 128),
        "gmk": p["mk_norm_g"].reshape(1, 128),
        "cmat": cmat,
        "alibi": al,
    }
    return {k_: np.ascontiguousarray(v_, dtype=f) for k_, v_ in out.items()}


_PARAM_NAMES = ("norm_g", "w_in", "conv_w", "conv_b", "dt_bias", "a_log", "d_skip", "ssd_norm_g", "q_norm_g",
                "k_norm_g", "sink", "mem_norm_g", "w_mem_kv", "mq_norm_g", "mk_norm_g", "w_out")


def run_layer(seq_groups, params, n_pairs):
    p0 = {k: np.asarray(params[k], np.float32)[0] for k in _PARAM_NAMES}
    segs = [(x.shape[1], x.shape[1] // 2) for x, m in seq_groups]
    nc = build_program(segs)
    cp = [_core_params(p0, False), _core_params(p0, True)]
    in_maps = []
    for c in range(2 * n_pairs):
        kk, par = c // 2, c % 2
        m = dict(cp[par])
        for i, (x, mem) in enumerate(seq_groups):
            xx = np.asarray(x[kk], np.float32)
            m[f"x{i}"] = np.ascontiguousarray(xx[::-1] if par else xx)
            m[f"mem{i}"] = np.ascontiguousarray(np.asarray(mem[kk], np.float32))
        in_maps.append(m)
    res = run_bass_kernel_spmd(nc, in_maps, core_ids=list(range(2 * n_pairs)))
    outs = []
    for i, (x, mem) in enumerate(seq_groups):
        T = x.shape[1]
        H = T // 2
        y = np.empty((n_pairs, T, D), np.float32)
        for c in range(2 * n_pairs):
            kk, par = c // 2, c % 2
            yc = res.results[c][f"y{i}"]
            if par:
                y[kk, H:] = yc[::-1]
            else:
                y[kk, :H] = yc
        outs.append(y)
    return outs


def kernel(x_prompt, x_sample, mem_prompt, mem_sample, **params):
    outs = run_layer([(np.asarray(x_sample), np.asarray(mem_sample)), (np.asarray(x_prompt), np.asarray(mem_prompt))],
                     params, 4)
    return (outs[1], outs[0])
```
